# Optimizing a Trainium2 kernel written in Bass

```python
import jax, jax.numpy as jnp
from jax import lax
import numpy as np

D_MODEL = 2048
BATCH = 4
SEQ = 2048
DEPTH = 2
DEC_BATCH = 128
DEC_SEQ = 8
PAST_LEN = 16384
PAGE_SIZE = 128

N_META = 16
N_MIXERS = 2
N_MLSTM_LAYERS = (DEPTH + 1) // 2
N_POOL_LAYERS = DEPTH // 2
MLSTM_HEADS = 4
MLSTM_DQK = D_MODEL // (2 * MLSTM_HEADS)
MLSTM_DV = D_MODEL // MLSTM_HEADS
MLSTM_CHUNK = 64
MLSTM_PROJ = 2 * MLSTM_HEADS * MLSTM_DQK + 2 * MLSTM_HEADS * MLSTM_DV + 2 * MLSTM_HEADS
MLSTM_SPLITS = (MLSTM_HEADS * MLSTM_DQK,
                2 * MLSTM_HEADS * MLSTM_DQK,
                2 * MLSTM_HEADS * MLSTM_DQK + MLSTM_HEADS * MLSTM_DV,
                2 * MLSTM_HEADS * MLSTM_DQK + 2 * MLSTM_HEADS * MLSTM_DV,
                2 * MLSTM_HEADS * MLSTM_DQK + 2 * MLSTM_HEADS * MLSTM_DV + MLSTM_HEADS)
POOL_WINDOWS = (2, 4, 8, 16)
POOL_GROUPS = len(POOL_WINDOWS)
POOL_GROUP_DIM = D_MODEL // POOL_GROUPS
POOL_BUF = max(POOL_WINDOWS) - 1
D_FF = 4 * D_MODEL
EPS = 1e-6
GATE_PAD = -1e30

kernel_name = 'hybrid_mlstm_pool_decoder_step'


def rmsnorm(x, g):
    xf = x.astype(jnp.float32)
    y = xf * lax.rsqrt(jnp.mean(xf * xf, axis=-1, keepdims=True) + EPS) * g.astype(jnp.float32)
    return y.astype(x.dtype)


def mlstm_chunkwise(q, k, v, log_i, log_f, state, chunk):
    B, T, H, _ = q.shape
    L = min(chunk, T)
    pad = (-T) % L
    if pad:
        pw = ((0, 0), (0, pad), (0, 0), (0, 0))
        q, k, v = jnp.pad(q, pw), jnp.pad(k, pw), jnp.pad(v, pw)
        log_i = jnp.pad(log_i, ((0, 0), (0, pad), (0, 0)), constant_values=GATE_PAD)
        log_f = jnp.pad(log_f, ((0, 0), (0, pad), (0, 0)))
    nc = (T + pad) // L

    def to_chunks(a):
        return jnp.moveaxis(a.reshape(B, nc, L, *a.shape[2:]), 1, 0)

    causal = jnp.tril(jnp.ones((L, L), dtype=bool))

    def step(carry, xs):
        C, n, m = carry
        qc, kc, vc, lic, lfc = xs
        b = jnp.cumsum(lfc, axis=1).transpose(0, 2, 1)
        li = lic.transpose(0, 2, 1)
        dmat = jnp.where(causal, b[..., :, None] - b[..., None, :] + li[..., None, :], -jnp.inf)
        m_inter = b + m[..., None]
        m_t = jnp.maximum(m_inter, jnp.max(dmat, axis=-1))
        w_inter = jnp.exp(m_inter - m_t)
        scores = jnp.einsum('blhd,bshd->bhls', qc, kc) * jnp.exp(dmat - m_t[..., None])
        num = (jnp.einsum('bhls,bshe->blhe', scores, vc)
               + jnp.einsum('blhd,bhde->blhe', qc, C) * w_inter.transpose(0, 2, 1)[..., None])
        den = jnp.sum(scores, axis=-1) + w_inter * jnp.einsum('blhd,bhd->bhl', qc, n)
        denom = jnp.maximum(jnp.abs(den), jnp.exp(-m_t)).transpose(0, 2, 1)[..., None]
        h = num / denom
        m_new = m_t[..., -1]
        decay = jnp.exp(b[..., -1:] - b + li - m_new[..., None])
        carry_w = jnp.exp(b[..., -1] + m - m_new)
        C_new = carry_w[..., None, None] * C + jnp.einsum('bhs,bshd,bshe->bhde', decay, kc, vc)
        n_new = carry_w[..., None] * n + jnp.einsum('bhs,bshd->bhd', decay, kc)
        return (C_new, n_new, m_new), h

    state, hs = lax.scan(step, state, (to_chunks(q), to_chunks(k), to_chunks(v),
                                       to_chunks(log_i), to_chunks(log_f)))
    h = jnp.moveaxis(hs, 0, 1).reshape(B, nc * L, H, MLSTM_DV)[:, :T]
    return h, state


def mlstm_mixer(xn, c0, n0, m0, segments, w_in, b_i, b_f, head_g, w_out):
    B, T, _ = xn.shape
    H = MLSTM_HEADS
    proj = xn @ w_in
    q, k, v, o, gi, gf = jnp.split(proj, MLSTM_SPLITS, axis=-1)
    q = q.reshape(B, T, H, MLSTM_DQK).astype(jnp.float32)
    k = k.reshape(B, T, H, MLSTM_DQK).astype(jnp.float32) * (MLSTM_DQK ** -0.5)
    v = v.reshape(B, T, H, MLSTM_DV).astype(jnp.float32)
    log_i = gi.astype(jnp.float32) + b_i.astype(jnp.float32)
    log_f = jax.nn.log_sigmoid(gf.astype(jnp.float32) + b_f.astype(jnp.float32))
    state = (c0.astype(jnp.float32), n0.astype(jnp.float32), m0.astype(jnp.float32))
    hs = []
    start = 0
    for length, chunk in segments:
        sl = slice(start, start + length)
        h, state = mlstm_chunkwise(q[:, sl], k[:, sl], v[:, sl], log_i[:, sl], log_f[:, sl], state, chunk)
        hs.append(h)
        start += length
    h = jnp.concatenate(hs, axis=1)
    hf = h * lax.rsqrt(jnp.mean(h * h, axis=-1, keepdims=True) + EPS) \
        * head_g.astype(jnp.float32).reshape(H, MLSTM_DV)
    gate = jax.nn.sigmoid(o.astype(jnp.float32)).reshape(B, T, H, MLSTM_DV)
    out = (hf * gate).reshape(B, T, H * MLSTM_DV).astype(xn.dtype) @ w_out
    return out, state


def pool_mixer(xn, prefix, start_pos, w_in, w_group, scale, w_out):
    B, T, _ = xn.shape
    u = xn @ w_in
    ext = jnp.concatenate([prefix.astype(u.dtype), u], axis=1)
    cs = jnp.pad(jnp.cumsum(ext.astype(jnp.float32), axis=1), ((0, 0), (1, 0), (0, 0)))
    pos = (start_pos + jnp.arange(T)).astype(jnp.float32)
    hi = cs[:, POOL_BUF + 1:POOL_BUF + 1 + T]
    outs = []
    for g, w in enumerate(POOL_WINDOWS):
        sl = slice(g * POOL_GROUP_DIM, (g + 1) * POOL_GROUP_DIM)
        total = hi[..., sl] - cs[:, POOL_BUF + 1 - w:POOL_BUF + 1 - w + T, sl]
        cnt = jnp.minimum(jnp.float32(w), pos + 1.0)[None, :, None]
        outs.append(total / cnt)
    pooled = jnp.concatenate(outs, axis=-1) - u.astype(jnp.float32)
    mixed = jnp.einsum('btgc,gcd->btgd', pooled.reshape(B, T, POOL_GROUPS, POOL_GROUP_DIM),
                       w_group.astype(jnp.float32)).reshape(B, T, D_MODEL)
    out = (mixed * scale.astype(jnp.float32)).astype(xn.dtype) @ w_out
    return out, ext[:, -POOL_BUF:]


def sq_relu_mlp(xn, w_up, w_down):
    return jnp.square(jax.nn.relu(xn @ w_up)) @ w_down


def setup_inputs(seed: int = 0) -> dict:
    key = jax.random.key(seed)
    ks = jax.random.split(key, 24)
    f32 = jnp.float32
    NA, NB, H = N_MLSTM_LAYERS, N_POOL_LAYERS, MLSTM_HEADS

    def nrm(k, shape, s):
        return jax.random.normal(k, shape, f32) * s

    return {
        'x_prompt': nrm(ks[0], (BATCH, SEQ, D_MODEL), 1.0),
        'x_sample': nrm(ks[1], (DEC_BATCH, DEC_SEQ, D_MODEL), 1.0),
        'state_mlstm_c': nrm(ks[2], (NA, DEC_BATCH, H, MLSTM_DQK, MLSTM_DV), 0.1),
        'state_mlstm_n': nrm(ks[3], (NA, DEC_BATCH, H, MLSTM_DQK), 0.1),
        'state_mlstm_m': nrm(ks[4], (NA, DEC_BATCH, H), 0.5),
        'state_pool': nrm(ks[5], (NB, DEC_BATCH, POOL_BUF, D_MODEL), 1.0),
        'meta_tokens': nrm(ks[6], (N_META, D_MODEL), 1.0),
        'norm_mix_pre': 1.0 + nrm(ks[7], (DEPTH, D_MODEL), 0.02),
        'norm_mix_post': 1.0 + nrm(ks[8], (DEPTH, D_MODEL), 0.02),
        'norm_ffn_pre': 1.0 + nrm(ks[9], (DEPTH, D_MODEL), 0.02),
        'norm_ffn_post': 1.0 + nrm(ks[10], (DEPTH, D_MODEL), 0.02),
        'mlstm_w_in': nrm(ks[11], (NA, D_MODEL, MLSTM_PROJ), D_MODEL ** -0.5),
        'mlstm_b_i': nrm(ks[12], (NA, H), 0.1),
        'mlstm_b_f': jnp.linspace(3.0, 6.0, H, dtype=f32)[None, :] + nrm(ks[13], (NA, H), 0.1),
        'mlstm_head_norm': 1.0 + nrm(ks[14], (NA, H * MLSTM_DV), 0.02),
        'mlstm_w_out': nrm(ks[15], (NA, H * MLSTM_DV, D_MODEL), (H * MLSTM_DV) ** -0.5),
        'pool_w_in': nrm(ks[16], (NB, D_MODEL, D_MODEL), D_MODEL ** -0.5),
        'pool_w_group': nrm(ks[17], (NB, POOL_GROUPS, POOL_GROUP_DIM, POOL_GROUP_DIM), POOL_GROUP_DIM ** -0.5),
        'pool_scale': 1.0 + nrm(ks[18], (NB, D_MODEL), 0.1),
        'pool_w_out': nrm(ks[19], (NB, D_MODEL, D_MODEL), D_MODEL ** -0.5),
        'ffn_w_up': nrm(ks[20], (DEPTH, D_MODEL, D_FF), D_MODEL ** -0.5),
        'ffn_w_down': nrm(ks[21], (DEPTH, D_FF, D_MODEL), D_FF ** -0.5),
    }


def reference(x_prompt, x_sample, state_mlstm_c, state_mlstm_n, state_mlstm_m, state_pool,
              meta_tokens, norm_mix_pre, norm_mix_post, norm_ffn_pre, norm_ffn_post,
              mlstm_w_in, mlstm_b_i, mlstm_b_f, mlstm_head_norm, mlstm_w_out,
              pool_w_in, pool_w_group, pool_scale, pool_w_out, ffn_w_up, ffn_w_down):
    B, S, _ = x_prompt.shape
    DB, DS, _ = x_sample.shape
    H = MLSTM_HEADS
    meta = jnp.broadcast_to(meta_tokens.astype(x_prompt.dtype)[None], (B, N_META, D_MODEL))
    hp = jnp.concatenate([meta, x_prompt], axis=1)
    hs = x_sample
    c_p, n_p, m_p, pool_p = [], [], [], []
    c_s, n_s, m_s, pool_s = [], [], [], []
    for layer in range(DEPTH):
        j = layer // N_MIXERS
        xn_p = rmsnorm(hp, norm_mix_pre[layer])
        xn_s = rmsnorm(hs, norm_mix_pre[layer])
        if layer % N_MIXERS == 0:
            wts = (mlstm_w_in[j], mlstm_b_i[j], mlstm_b_f[j], mlstm_head_norm[j], mlstm_w_out[j])
            zc = jnp.zeros((B, H, MLSTM_DQK, MLSTM_DV), jnp.float32)
            zn = jnp.zeros((B, H, MLSTM_DQK), jnp.float32)
            zm = jnp.zeros((B, H), jnp.float32)
            mix_p, (cp, np_, mp) = mlstm_mixer(xn_p, zc, zn, zm,
                                               [(N_META, N_META), (S, MLSTM_CHUNK)], *wts)
            mix_s, (cs_, ns_, ms_) = mlstm_mixer(xn_s, state_mlstm_c[j], state_mlstm_n[j], state_mlstm_m[j],
                                                 [(DS, MLSTM_CHUNK)], *wts)
            c_p.append(cp.astype(state_mlstm_c.dtype))
            n_p.append(np_.astype(state_mlstm_n.dtype))
            m_p.append(mp.astype(state_mlstm_m.dtype))
            c_s.append(cs_.astype(state_mlstm_c.dtype))
            n_s.append(ns_.astype(state_mlstm_n.dtype))
            m_s.append(ms_.astype(state_mlstm_m.dtype))
        else:
            wts = (pool_w_in[j], pool_w_group[j], pool_scale[j], pool_w_out[j])
            mix_p, bp = pool_mixer(xn_p, jnp.zeros((B, POOL_BUF, D_MODEL), xn_p.dtype), 0, *wts)
            mix_s, bs = pool_mixer(xn_s, state_pool[j], PAST_LEN, *wts)
            pool_p.append(bp.astype(state_pool.dtype))
            pool_s.append(bs.astype(state_pool.dtype))
        hp = hp + rmsnorm(mix_p, norm_mix_post[layer])
        hs = hs + rmsnorm(mix_s, norm_mix_post[layer])
        hp = hp + rmsnorm(sq_relu_mlp(rmsnorm(hp, norm_ffn_pre[layer]), ffn_w_up[layer], ffn_w_down[layer]),
                          norm_ffn_post[layer])
        hs = hs + rmsnorm(sq_relu_mlp(rmsnorm(hs, norm_ffn_pre[layer]), ffn_w_up[layer], ffn_w_down[layer]),
                          norm_ffn_post[layer])
    y_prompt = hp[:, N_META:]
    y_sample = hs
    new_c_prompt = jnp.stack(c_p, 0)
    new_n_prompt = jnp.stack(n_p, 0)
    new_m_prompt = jnp.stack(m_p, 0)
    new_pool_prompt = jnp.stack(pool_p, 0)
    new_c_sample = jnp.stack(c_s, 0)
    new_n_sample = jnp.stack(n_s, 0)
    new_m_sample = jnp.stack(m_s, 0)
    new_pool_sample = jnp.stack(pool_s, 0)
    return (y_prompt, y_sample, new_c_prompt, new_n_prompt, new_m_prompt, new_pool_prompt,
            new_c_sample, new_n_sample, new_m_sample, new_pool_sample)
```

```python
import numpy as np
from contextlib import ExitStack
import concourse.bass as bass
import concourse.mybir as mybir
from concourse.bass_utils import run_bass_kernel_spmd

F32, BF16 = mybir.dt.float32, mybir.dt.bfloat16
AF = mybir.ActivationFunctionType
ALU = mybir.AluOpType

D = 2048
NT = 1168
NPRE = 1024
EPS = 1e-6
DFF = 8192
TILES = [(16 + 128 * t, 128) for t in range(8)] + [(1040, 128), (0, 16)]
TGS = [(0, 512), (512, 512), (1024, 144)]
ENGS = ("pe", "act", "dve", "pool", "sp")


class _Op:
    __slots__ = ("eng", "fn", "deps", "is_dma", "sig", "sem", "semval", "idx")
    _ctr = [0]

    def __init__(self, eng, fn, is_dma):
        self.eng, self.fn, self.is_dma = eng, fn, is_dma
        self.deps, self.sig, self.sem, self.semval = [], False, None, None
        _Op._ctr[0] += 1
        self.idx = _Op._ctr[0]


class Sched:
    def __init__(self, nc, n_dma_sems=60):
        self.nc = nc
        self.ops = {e: [] for e in ENGS}
        self.last_w, self.readers = {}, {}
        self.n_dma_sems = n_dma_sems
        self.since_barrier = []

    def _add(self, eng, fn, reads, writes, is_dma):
        op = _Op(eng, fn, is_dma)
        deps = set()
        for k in reads:
            w = self.last_w.get(k)
            if w is not None:
                deps.add(w)
        for k in writes:
            w = self.last_w.get(k)
            if w is not None:
                deps.add(w)
            for r in self.readers.get(k, ()):
                deps.add(r)
        best = {}
        for d in deps:
            if d.eng == eng and not d.is_dma and not is_dma and eng == "pe":
                continue
            if d.is_dma or d.fn is None:
                op.deps.append(d)
            else:
                b_ = best.get(d.eng)
                if b_ is None or d.idx > b_.idx:
                    best[d.eng] = d
        op.deps.extend(best.values())
        for k in reads:
            lst = self.readers.setdefault(k, [])
            if not is_dma:
                lst[:] = [r for r in lst if r.is_dma or r.eng != eng]
            lst.append(op)
        for k in writes:
            self.last_w[k] = op
            self.readers[k] = []
        self.ops[eng].append(op)
        if is_dma:
            self.since_barrier.append(op)
        return op

    def op(self, eng, fn, reads=(), writes=()):
        return self._add(eng, fn, tuple(reads), tuple(writes), False)

    def dma(self, eng, fn, reads=(), writes=()):
        return self._add(eng, fn, tuple(reads), tuple(writes), True)

    def barrier(self):
        lasts = []
        for e in ENGS:
            for o in reversed(self.ops[e]):
                if not o.is_dma and o.fn is not None:
                    lasts.append(o)
                    break
        lasts += self.since_barrier
        self.since_barrier = []
        for e in ENGS:
            o = _Op(e, None, False)
            o.deps = [d for d in lasts]
            self.ops[e].append(o)
        self.last_w, self.readers = {}, {}

    def emit(self):
        nc = self.nc
        for e in ENGS:
            for op in self.ops[e]:
                for d in op.deps:
                    d.sig = True
        with ExitStack() as st:
            esem = {e: st.enter_context(nc.semaphore("prog_" + e)) for e in ENGS}
            dsems = [st.enter_context(nc.semaphore("dma%d" % i)) for i in range(self.n_dma_sems)]
            for e in ENGS:
                cnt = 0
                for op in self.ops[e]:
                    if op.is_dma or op.fn is None:
                        continue
                    if op.sig:
                        cnt += 1
                        op.sem, op.semval = esem[e], cnt
            dcount = [0] * self.n_dma_sems
            dlast = [None] * self.n_dma_sems
            dma_prev = {}
            dma_engs = [e for e in ENGS if any(o.is_dma for o in self.ops[e])]
            per = self.n_dma_sems // max(1, len(dma_engs))
            for ei, e in enumerate(dma_engs):
                nxt = 0
                for op in self.ops[e]:
                    if not op.is_dma:
                        continue
                    s = ei * per + nxt
                    nxt = (nxt + 1) % per
                    if dlast[s] is not None:
                        dma_prev[op] = dlast[s]
                    dcount[s] += 16
                    dlast[s] = op
                    op.sem, op.semval, op.sig = dsems[s], dcount[s], True
            block = st.enter_context(nc.Block())
            engobj = {"pe": "tensor", "act": "scalar", "dve": "vector", "pool": "gpsimd", "sp": "sync"}

            def make(e):
                def body(eng):
                    waited = {}
                    for op in self.ops[e]:
                        deps = list(op.deps)
                        if op in dma_prev:
                            deps.append(dma_prev[op])
                        need = {}
                        for d in deps:
                            if d.sem is None:
                                continue
                            key = id(d.sem)
                            if need.get(key, (None, 0))[1] < d.semval:
                                need[key] = (d.sem, d.semval)
                        for key, (sem, val) in need.items():
                            if waited.get(key, 0) >= val:
                                continue
                            eng.wait_ge(sem, val)
                            waited[key] = val
                        if op.fn is None:
                            continue
                        ins = op.fn(eng)
                        if ins is None:
                            continue
                        if op.sig:
                            ins.then_inc(op.sem, 16 if op.is_dma else 1)
                return body

            for e in ENGS:
                if self.ops[e]:
                    getattr(block, engobj[e])(make(e))


def build_nc(stage=99):
    nc = bass.Bass("TRN2", target_bir_lowering=False)

    def din(name, shape):
        return nc.dram_tensor(name, list(shape), F32, kind="ExternalInput").ap()

    def dout(name, shape):
        return nc.dram_tensor(name, list(shape), F32, kind="ExternalOutput").ap()

    xin = din("xin", [NT, D])
    xpre = din("xpre", [NPRE, D])
    sc = din("sc", [16, 4, 256, 512])
    sn = din("sn", [16, 4, 256])
    smT = din("smT", [4, 16])
    spool = din("spool", [16, 15, D])
    gpre_fm = din("gpre_fm", [128, 4, 16])
    gpost = din("gpost", [4, D])
    w_in0 = din("w_in0", [D, 6152])
    gb_i = din("gb_i", [4, 1])
    gb_f = din("gb_f", [4, 1])
    headg = din("headg", [128, 16])
    w_out0 = din("w_out0", [D, D])
    pw_in = din("pw_in", [D, D])
    pw_group = din("pw_group", [4, 512, 512])
    pscale_fm = din("pscale_fm", [128, 16])
    pw_out = din("pw_out", [D, D])
    w_up = din("w_up", [2, D, DFF])
    w_down = din("w_down", [2, DFF, D])
    c_ident = din("c_ident", [128, 128])
    c_maskU = din("c_maskU", [128, 128])
    c_maskS = din("c_maskS", [128, 128])
    c_ind = din("c_ind", [128, 16])

    y = dout("y", [1152, D])
    o_cp = dout("o_cp", [4, 256, 512])
    o_np = dout("o_np", [4, 256])
    o_mp = dout("o_mp", [4, 1])
    o_poolp = dout("o_poolp", [16, D])
    o_cs = dout("o_cs", [16, 4, 256, 512])
    o_ns = dout("o_ns", [16, 4, 256])
    o_msT = dout("o_msT", [4, 16])
    o_pools = dout("o_pools", [16, 15, D])
    hscr = nc.dram_tensor("hscr", [NT, D], F32, kind="Internal").ap()

    with ExitStack() as st:
        def sb(name, shape, dt):
            return st.enter_context(nc.sbuf_tensor(name, list(shape), dt))

        ACC = sb("ACC", [128, 20480], F32)
        XNT = sb("XNT", [128, 16, NT], BF16)
        R2 = sb("R2", [128, 9344], F32)
        WB = [sb("WB%d" % i, [128, 8192], BF16) for i in range(2)]
        MISC = sb("MISC", [128, 3072], F32)
        ident = sb("ident", [128, 128], F32)
        identb = sb("identb", [128, 128], BF16)
        maskU = sb("maskU", [128, 128], BF16)
        maskS = sb("maskS", [128, 128], BF16)
        ind = sb("ind", [128, 16], F32)
        gpre = sb("gpre", [128, 4, 16], F32)
        pscale = sb("pscale", [128, 16], F32)
        small = sb("small", [128, 64], F32)
        banks = [st.enter_context(nc.psum_tensor("bank%d" % i, [128, 512], F32)) for i in range(8)]

        acc = ACC[:].rearrange("p (t d) -> p t d", t=10)
        S = Sched(nc)
        bank_ctr = [0]

        bank_excl = set()

        bank_pool = [list(range(8))]

        def nbank():
            pool_ = bank_pool[0]
            i = pool_[bank_ctr[0] % len(pool_)]
            bank_ctr[0] += 1
            return i

        wb_ctr = [0]

        pending_pref = {}

        def prefetch_w(tag, src2d, R, C):
            dst = WB[0][:, 0:R * C].rearrange("p (k c) -> p k c", k=R)
            S.dma("pool", lambda e: e.dma_start(out=dst, in_=src2d.rearrange("(k p) c -> p k c", p=128)), writes=["wb0"])
            pending_pref[tag] = (dst, "wb0")

        def load_w(src2d, R, C, tag=None):
            if tag is not None and tag in pending_pref:
                wb_ctr[0] = 1
                return pending_pref.pop(tag)
            i = wb_ctr[0] % 2
            wb_ctr[0] += 1
            dst = WB[i][:, 0:R * C].rearrange("p (k c) -> p k c", k=R)
            S.dma("pool", lambda e: e.dma_start(out=dst, in_=src2d.rearrange("(k p) c -> p k c", p=128)),
                  writes=["wb%d" % i])
            return dst, "wb%d" % i

        S.dma("sp", lambda e: e.dma_start(out=ident[:], in_=c_ident[:, :]), writes=["ident"])
        S.dma("pool", lambda e: e.dma_start(out=identb[:], in_=c_ident[:, :]), writes=["identb"])
        S.dma("pool", lambda e: e.dma_start(out=maskU[:], in_=c_maskU[:, :]), writes=["maskU"])
        S.dma("pool", lambda e: e.dma_start(out=maskS[:], in_=c_maskS[:, :]), writes=["maskS"])
        S.dma("sp", lambda e: e.dma_start(out=ind[:], in_=c_ind[:, :]), writes=["ind"])
        S.dma("sp", lambda e: e.dma_start(out=gpre[:], in_=gpre_fm[:, :, :]), writes=["gpre"])
        S.dma("sp", lambda e: e.dma_start(out=pscale[:], in_=pscale_fm[:, :]), writes=["pscale"])

        xt = R2[:, 0:2048]
        xs = R2[:, 2048:4096]
        grow = R2[:, 4096:6144]
        tmpb = R2[:, 6144:8192]
        junk = R2[:, 8192:9216].bitcast(BF16)
        WB1f = WB[1][:].bitcast(F32)
        epsc = small[:, 40:41]
        S.op("dve", lambda e: e.memset(epsc, EPS), writes=["epsc"])
        par = [0]
        mode = ["A"]

        def tset():
            p = par[0] % 2
            par[0] += 1
            if p == 0:
                a, b_ = R2[:, 0:2048], R2[:, 2048:4096]
            elif mode[0] == "A":
                a, b_ = R2[:, 4096:6144], R2[:, 6144:8192]
            else:
                a, b_ = WB1f[:, 0:2048], WB1f[:, 2048:4096]
            o8 = 8 * p
            return a, b_, small[:, o8:o8 + 1], small[:, o8 + 1:o8 + 2], small[:, o8 + 2:o8 + 3], str(p)

        def rstd_of(src, n, srckeys, T):
            _, _, ss, lv, rstd, sx = T
            S.op("act", lambda e: e.activation(out=junk[:n], in_=src, func=AF.Square, accum_out=ss[:n]),
                 reads=srckeys, writes=["ss" + sx])
            S.op("act", lambda e: e.activation(out=lv[:n], in_=ss[:n], func=AF.Ln, bias=epsc[:n], scale=1.0 / D),
                 reads=["ss" + sx, "epsc"], writes=["lv" + sx])
            S.op("act", lambda e: e.activation(out=rstd[:n], in_=lv[:n], func=AF.Exp, scale=-0.5),
                 reads=["lv" + sx], writes=["rstd" + sx])

        def to_fm_a(src, n, srckeys, T):
            _, xs_, _, _, rstd, sx = T
            S.op("dve", lambda e: e.tensor_scalar(out=xs_[:n], in0=src, scalar1=rstd[:n], scalar2=None, op0=ALU.mult),
                 reads=list(srckeys) + ["rstd" + sx], writes=["xs" + sx])

        def to_fm_b(n, dstT, col0, gidx, dstkey, T):
            _, xs_, _, _, rstd, sx = T
            for g4 in range(4):
                b = nbank()
                pt = banks[b][:].rearrange("p (j n) -> p j n", j=4)
                for j in range(4):
                    c = g4 * 4 + j
                    S.op("pe", (lambda c=c, j=j, pt=pt: lambda e: e.transpose(out=pt[:, j, :n], in_=xs_[:n, c * 128:(c + 1) * 128], identity=ident[:n, :n]))(),
                         reads=["xs" + sx, "ident"], writes=["pb%d" % b])
                for j in range(4):
                    c = g4 * 4 + j
                    fk = "%s_%d_%d" % (dstkey, c, col0)
                    if g4 == 0 or (mode[0] == "B" and g4 == 1):
                        S.op("act", (lambda c=c, j=j, pt=pt: lambda e: e.activation(out=dstT[:, c, col0:col0 + n], in_=pt[:, j, :n], func=AF.Copy, scale=gpre[:, gidx, c:c + 1]))(),
                             reads=["pb%d" % b, "gpre"], writes=[fk])
                    else:
                        S.op("dve", (lambda c=c, j=j, pt=pt: lambda e: e.tensor_scalar(out=dstT[:, c, col0:col0 + n], in0=pt[:, j, :n], scalar1=gpre[:, gidx, c:c + 1], scalar2=None, op0=ALU.mult))(),
                             reads=["pb%d" % b, "gpre"], writes=[fk])

        def pipeline(stages):
            prev = None
            for (sa, sb_) in stages:
                sa()
                if prev is not None:
                    prev()
                prev = sb_
            if prev is not None:
                prev()

        def prenorm_stage(src_rows, n, dstT, col0, gidx, dstkey):
            T = tset()
            xt_, sx = T[0], T[5]

            def sa():
                S.dma("sp", lambda e: e.dma_start(out=xt_[:n], in_=src_rows), writes=["xt" + sx])
                rstd_of(xt_[:n], n, ["xt" + sx], T)
                to_fm_a(xt_[:n], n, ["xt" + sx], T)

            def sb_():
                to_fm_b(n, dstT, col0, gidx, dstkey, T)
            return sa, sb_

        def pipeline3(stages):
            n_ = len(stages)
            for i in range(n_ + 2):
                if i < n_:
                    stages[i][0]()
                if 0 <= i - 1 < n_:
                    stages[i - 1][1]()
                if 0 <= i - 2 < n_:
                    stages[i - 2][2]()

        def boundary_stage(t, n, hsrc_rows, hdst_rows, ydst_rows, post_idx, next_gidx, dstT, col0):
            T = tset()
            sx = T[5]
            o8 = 8 * int(sx)
            T2 = (T[0], T[1], small[:, o8 + 3:o8 + 4], small[:, o8 + 4:o8 + 5], small[:, o8 + 5:o8 + 6], sx + "b")
            rstd = T[4]
            a = acc[:n, t, :]

            xt_ = T[0]

            def s1():
                S.dma("sp", lambda e: e.dma_start(out=xt_[:n], in_=hsrc_rows), writes=["xt" + sx])
                if post_idx in (0, 2):
                    ss_, lv_ = T[2], T[3]
                    S.op("dve", lambda e: e.tensor_reduce(out=ss_[:n], in_=ssp[:n, 4 * t:4 * t + 4], axis=mybir.AxisListType.X, op=ALU.add),
                         reads=["ssp%d" % t], writes=["ss" + sx])
                    S.op("act", lambda e: e.activation(out=lv_[:n], in_=ss_[:n], func=AF.Ln, bias=epsc[:n], scale=1.0 / D),
                         reads=["ss" + sx, "epsc"], writes=["lv" + sx])
                    S.op("act", lambda e: e.activation(out=rstd[:n], in_=lv_[:n], func=AF.Exp, scale=-0.5),
                         reads=["lv" + sx], writes=["rstd" + sx])
                    S.op("dve", lambda e: e.scalar_tensor_tensor(out=xt_[:n], in0=a, scalar=rstd[:n], in1=xt_[:n], op0=ALU.mult, op1=ALU.add),
                         reads=["acc%d" % t, "rstd" + sx, "xt" + sx], writes=["xt" + sx])
                    return
                rstd_of(a, n, ["acc%d" % t], T)
                S.op("dve", lambda e: e.scalar_tensor_tensor(out=a, in0=a, scalar=rstd[:n], in1=grow[:n], op0=ALU.mult, op1=ALU.mult),
                     reads=["acc%d" % t, "rstd" + sx, "grow"], writes=["acc%d" % t])
                S.op("pool", lambda e: e.tensor_tensor(out=xt_[:n], in0=a, in1=xt_[:n], op=ALU.add),
                     reads=["acc%d" % t, "xt" + sx], writes=["xt" + sx])

            def s2():
                if hdst_rows is not None:
                    S.dma("sp", lambda e: e.dma_start(out=hdst_rows, in_=xt_[:n]), reads=["xt" + sx], writes=["hscr%d" % t])
                if ydst_rows is not None:
                    S.dma("sp", lambda e: e.dma_start(out=ydst_rows, in_=xt_[:n]), reads=["xt" + sx], writes=["yout"])
                if next_gidx is not None:
                    rstd_of(xt_[:n], n, ["xt" + sx], T2)
                    xs_ = T2[1]
                    S.op("dve", lambda e: e.tensor_scalar(out=xs_[:n], in0=xt_[:n], scalar1=T2[4][:n], scalar2=None, op0=ALU.mult),
                         reads=["xt" + sx, "rstd" + sx + "b"], writes=["xs" + sx])

            def s3():
                if next_gidx is not None:
                    to_fm_b(n, dstT, col0, next_gidx, "xnT", T)
            return s1, s2, s3

        def load_grow(idx):
            S.dma("sp", lambda e: e.dma_start(out=grow, in_=gpost[idx:idx + 1, :].partition_broadcast(128)[:, 0, :]),
                  writes=["grow"])

        growM = MISC[:, 0:2048]
        junkM = MISC[:, 2048:2304].bitcast(BF16)
        ssp = sb("ssp", [128, 40], F32)

        def proj_to_acc(actT, actkey, wsrc, tiles, tag=None):
            for cg in range(4):
                wv, wk = load_w(wsrc[:, cg * 512:(cg + 1) * 512], 16, 512, tag=(tag if cg == 0 else None))
                for t in tiles:
                    col0, n = TILES[t]
                    b = nbank()
                    for kc in range(16):
                        S.op("pe", (lambda kc=kc, b=b, col0=col0, n=n, wv=wv: lambda e: e.matmul(banks[b][:n, :], lhsT=actT[:, kc, col0:col0 + n], rhs=wv[:, kc, :], start=(kc == 0), stop=(kc == 15)))(),
                             reads=[actkey, wk], writes=["pb%d" % b])
                    S.op("act", (lambda b=b, t=t, n=n, cg=cg: lambda e: e.activation(out=junkM[:n], in_=banks[b][:n, :], func=AF.Square, accum_out=ssp[:n, 4 * t + cg:4 * t + cg + 1]))(),
                         reads=["pb%d" % b], writes=["ssp%d" % t, "ser%d" % b])
                    S.op("dve", (lambda b=b, t=t, n=n, cg=cg: lambda e: e.tensor_tensor(out=acc[:n, t, cg * 512:(cg + 1) * 512], in0=banks[b][:n, :], in1=growM[:n, cg * 512:(cg + 1) * 512], op=ALU.mult))(),
                         reads=["pb%d" % b, "ser%d" % b, "growM"], writes=["acc%d" % t])

        def ffn(layer, tiles):
            hid = R2[:, 0:2336].bitcast(BF16).rearrange("p (c n) -> p c n", c=4)
            wd_all = R2[:, 2336:2336 + 6144].bitcast(BF16).rearrange("p (s k c) -> p s k c", s=6, k=4)
            wd_ctr = [0]
            for g in range(16):
                wu, wuk = load_w(w_up[layer, :, g * 512:(g + 1) * 512], 16, 512, tag=("up%d" % layer if g == 0 else None))
                for fc in range(4):
                    for (c0, ncol) in TGS:
                        b = nbank()
                        for kc in range(16):
                            S.op("pe", (lambda kc=kc, b=b, c0=c0, ncol=ncol, fc=fc, wu=wu: lambda e: e.matmul(banks[b][:, :ncol], lhsT=wu[:, kc, fc * 128:(fc + 1) * 128], rhs=XNT[:, kc, c0:c0 + ncol], start=(kc == 0), stop=(kc == 15)))(),
                                 reads=["xnT", wuk], writes=["pb%d" % b])
                        S.op("act", (lambda b=b, c0=c0, ncol=ncol, fc=fc: lambda e: e.activation(out=hid[:, fc, c0:c0 + ncol], in_=banks[b][:, :ncol], func=AF.Relu))(),
                             reads=["pb%d" % b], writes=["hid%d" % fc])
                        S.op("act", (lambda c0=c0, ncol=ncol, fc=fc: lambda e: e.activation(out=hid[:, fc, c0:c0 + ncol], in_=hid[:, fc, c0:c0 + ncol], func=AF.Square))(),
                             reads=["hid%d" % fc], writes=["hid%d" % fc])
                for cg in range(4):
                    s = wd_ctr[0] % 6
                    wd_ctr[0] += 1
                    wdv = wd_all[:, s]
                    S.dma("pool", (lambda wdv=wdv, g=g, cg=cg: lambda e: e.dma_start(out=wdv, in_=w_down[layer, g * 512:(g + 1) * 512, cg * 512:(cg + 1) * 512].rearrange("(k p) c -> p k c", p=128)))(),
                          writes=["wd%d" % s])
                    for t in tiles:
                        col0, n = TILES[t]
                        b = nbank()
                        for kc in range(4):
                            S.op("pe", (lambda kc=kc, b=b, col0=col0, n=n, wdv=wdv: lambda e: e.matmul(banks[b][:n, :], lhsT=hid[:, kc, col0:col0 + n], rhs=wdv[:, kc, :], start=(kc == 0), stop=(kc == 3)))(),
                                 reads=["hid%d" % kc, "wd%d" % s], writes=["pb%d" % b])
                        a = acc[:n, t, cg * 512:(cg + 1) * 512]
                        if g == 0:
                            S.op("dve", (lambda b=b, a=a, n=n: lambda e: e.tensor_copy(out=a, in_=banks[b][:n, :]))(),
                                 reads=["pb%d" % b], writes=["acc%d_%d" % (t, cg)])
                        else:
                            S.op("dve", (lambda b=b, a=a, n=n: lambda e: e.tensor_tensor(out=a, in0=a, in1=banks[b][:n, :], op=ALU.add))(),
                                 reads=["pb%d" % b, "acc%d_%d" % (t, cg)], writes=["acc%d_%d" % (t, cg)])


        def mm(out, lhsT, rhs, start, stop, reads, writes):
            S.op("pe", lambda e: e.matmul(out, lhsT=lhsT, rhs=rhs, start=start, stop=stop), reads=reads, writes=writes)

        XPRE = ACC[:, 0:8192].bitcast(BF16).rearrange("p (c n) -> p c n", c=16)
        pipeline([prenorm_stage(xin[128 * t:128 * t + TILES[t][1], :], TILES[t][1], XNT, TILES[t][0], 0, "xnT") for t in range(10)]
                 + [prenorm_stage(xpre[128 * t:128 * (t + 1), :], 128, XPRE, 128 * t, 0, "xpre") for t in range(8)])
        S.barrier()

        NG = 2192
        LI = ACC[0:4, 8192:8192 + NG]
        LF = ACC[0:4, 10384:10384 + NG]
        BB = ACC[0:4, 12576:12576 + NG]
        GG = ACC[0:4, 14768:14768 + NG]
        EE = ACC[0:4, 16960:16960 + NG]
        ZER = MISC[0:4, 0:NG]
        gsm = sb("gsm", [4, 256], F32)
        gbi, gbf, nbf = gsm[:, 0:1], gsm[:, 1:2], gsm[:, 2:3]
        smt = gsm[:, 4:20]
        NGE = gsm[:, 20:52]
        CW = gsm[:, 52:61]
        CS = gsm[:, 64:80]
        MP = gsm[:, 80:81]
        MS = gsm[:, 84:100]
        XD = gsm[:, 100:136]
        XDS = gsm[:, 136:200]
        ones4 = sb("ones4", [4, 128], F32)
        dectm = sb("dectm", [128, 18, 4], F32)
        floortm = sb("floortm", [128, 10, 4], F32)
        cwb = sb("cwb", [128, 36], F32)
        csb = sb("csb", [128, 64], F32)
        S.dma("sp", lambda e: e.dma_start(out=gbi, in_=gb_i[:, :]), writes=["gsm"])
        S.dma("sp", lambda e: e.dma_start(out=gbf, in_=gb_f[:, :]), writes=["gsm"])
        S.dma("sp", lambda e: e.dma_start(out=smt, in_=smT[:, :]), writes=["gsm"])
        S.op("dve", lambda e: e.tensor_scalar(out=nbf, in0=gbf, scalar1=-1.0, scalar2=None, op0=ALU.mult), reads=["gsm"], writes=["gsm"])
        S.op("dve", lambda e: e.memset(ZER, 0.0), writes=["zer"])
        S.op("dve", lambda e: e.memset(ones4[:], 1.0), writes=["ones4"])
        wg, wgk = load_w(w_in0[:, 6144:6152], 16, 8)

        def gate_proj(actT, actkey, c0, ncol, gcol):
            b1, b2 = nbank(), nbank()
            for kc in range(16):
                mm(banks[b1][0:4, :ncol], wg[:, kc, 0:4], actT[:, kc, c0:c0 + ncol], kc == 0, kc == 15, [actkey, wgk], ["pb%d" % b1])
            for kc in range(16):
                mm(banks[b2][0:4, :ncol], wg[:, kc, 4:8], actT[:, kc, c0:c0 + ncol], kc == 0, kc == 15, [actkey, wgk], ["pb%d" % b2])
            S.op("act", lambda e: e.activation(out=LI[:, gcol:gcol + ncol], in_=banks[b1][0:4, :ncol], func=AF.Identity, bias=gbi),
                 reads=["pb%d" % b1, "gsm"], writes=["LI"])
            S.op("act", lambda e: e.activation(out=LF[:, gcol:gcol + ncol], in_=banks[b2][0:4, :ncol], func=AF.Exp, bias=nbf, scale=-1.0),
                 reads=["pb%d" % b2, "gsm"], writes=["LF"])

        for (c0, ncol) in TGS:
            gate_proj(XNT, "xnT", c0, ncol, 1024 + c0)
        for c0 in (0, 512):
            gate_proj(XPRE, "xpre", c0, 512, c0)
        S.op("dve", lambda e: e.tensor_scalar(out=LF, in0=LF, scalar1=1.0, scalar2=None, op0=ALU.add), reads=["LF"], writes=["LF"])
        S.op("act", lambda e: e.activation(out=LF, in_=LF, func=AF.Ln), reads=["LF"], writes=["LF"])
        S.op("dve", lambda e: e.tensor_tensor_scan(out=BB[:, 0:2064], data0=LF[:, 0:2064], data1=ZER[:, 0:2064], initial=0.0, op0=ALU.add, op1=ALU.add),
             reads=["LF", "zer"], writes=["BB"])
        for q in range(16):
            a0 = 2064 + 8 * q
            S.op("dve", (lambda a0=a0: lambda e: e.tensor_tensor_scan(out=BB[:, a0:a0 + 8], data0=LF[:, a0:a0 + 8], data1=ZER[:, 0:8], initial=0.0, op0=ALU.add, op1=ALU.add))(),
                 reads=["LF", "zer"], writes=["BB"])
        S.op("dve", lambda e: e.tensor_tensor(out=LI, in0=LI, in1=BB, op=ALU.add), reads=["LI", "BB"], writes=["LI"])
        S.op("dve", lambda e: e.tensor_tensor_scan(out=GG[:, 0:2064], data0=LI[:, 0:2064], data1=LI[:, 0:2064], initial=0.0, op0=ALU.max, op1=ALU.max),
             reads=["LI"], writes=["GG"])
        for q in range(16):
            a0 = 2064 + 8 * q
            S.op("dve", (lambda a0=a0, q=q: lambda e: e.tensor_tensor_scan(out=GG[:, a0:a0 + 8], data0=LI[:, a0:a0 + 8], data1=LI[:, a0:a0 + 8], initial=smt[:, q:q + 1], op0=ALU.max, op1=ALU.max))(),
                 reads=["LI", "gsm"], writes=["GG"])
        CH = [(0, 1024), (1024, 16)] + [(1040 + 128 * t, 128) for t in range(8)] + [(2064 + 8 * q, 8) for q in range(16)]
        S.op("dve", lambda e: e.tensor_scalar(out=NGE[:, 0:1], in0=GG[:, 1023:1024], scalar1=-1.0, scalar2=None, op0=ALU.mult), reads=["GG"], writes=["NGE"])
        S.op("dve", lambda e: e.tensor_scalar(out=NGE[:, 1:2], in0=GG[:, 1039:1040], scalar1=-1.0, scalar2=None, op0=ALU.mult), reads=["GG"], writes=["NGE"])
        S.op("dve", lambda e: e.tensor_scalar(out=NGE[:, 2:10], in0=GG[:, 1040:2064].rearrange("p (j n) -> p j n", n=128)[:, :, 127], scalar1=-1.0, scalar2=None, op0=ALU.mult), reads=["GG"], writes=["NGE"])
        S.op("dve", lambda e: e.tensor_scalar(out=NGE[:, 10:26], in0=GG[:, 2064:2192].rearrange("p (q r) -> p q r", r=8)[:, :, 7], scalar1=-1.0, scalar2=None, op0=ALU.mult), reads=["GG"], writes=["NGE"])
        def chunk_bias_add(dst, src, key_r, key_w, extra=()):
            S.op("dve", lambda e: e.tensor_scalar(out=dst[:, 0:1024], in0=src[:, 0:1024], scalar1=NGE[:, 0:1], scalar2=None, op0=ALU.add), reads=[key_r, "NGE"] + list(extra), writes=[key_w])
            S.op("dve", lambda e: e.tensor_scalar(out=dst[:, 1024:1040], in0=src[:, 1024:1040], scalar1=NGE[:, 1:2], scalar2=None, op0=ALU.add), reads=[key_r, "NGE"] + list(extra), writes=[key_w])
            S.op("dve", lambda e: e.tensor_tensor(out=dst[:, 1040:2064].rearrange("p (j n) -> p j n", n=128), in0=src[:, 1040:2064].rearrange("p (j n) -> p j n", n=128),
                                                  in1=NGE[:, 2:10].unsqueeze(2).to_broadcast([4, 8, 128]), op=ALU.add), reads=[key_r, "NGE"] + list(extra), writes=[key_w])
            S.op("dve", lambda e: e.tensor_tensor(out=dst[:, 2064:2192].rearrange("p (q r) -> p q r", r=8), in0=src[:, 2064:2192].rearrange("p (q r) -> p q r", r=8),
                                                  in1=NGE[:, 10:26].unsqueeze(2).to_broadcast([4, 16, 8]), op=ALU.add), reads=[key_r, "NGE"] + list(extra), writes=[key_w])
        chunk_bias_add(EE, LI, "LI", "EE")
        S.op("act", lambda e: e.activation(out=EE, in_=EE, func=AF.Exp), reads=["EE"], writes=["EE"])
        chunk_bias_add(LF, BB, "BB", "FL", extra=["LF"])
        S.op("act", lambda e: e.activation(out=LF, in_=LF, func=AF.Exp), reads=["FL"], writes=["FL"])
        S.op("dve", lambda e: e.tensor_tensor(out=CW, in0=NGE[:, 1:10], in1=NGE[:, 0:9], op=ALU.subtract), reads=["NGE"], writes=["CW"])
        S.op("act", lambda e: e.activation(out=CW, in_=CW, func=AF.Exp), reads=["CW"], writes=["CW"])
        S.op("dve", lambda e: e.tensor_tensor(out=CS, in0=NGE[:, 10:26], in1=smt, op=ALU.add), reads=["NGE", "gsm"], writes=["CS"])
        S.op("act", lambda e: e.activation(out=CS, in_=CS, func=AF.Exp), reads=["CS"], writes=["CS"])
        S.op("dve", lambda e: e.scalar_tensor_tensor(out=MP, in0=NGE[:, 9:10], scalar=-1.0, in1=BB[:, 2063:2064], op0=ALU.mult, op1=ALU.subtract),
             reads=["NGE", "BB"], writes=["MP"])
        S.dma("sp", lambda e: e.dma_start(out=o_mp[:, :], in_=MP), reads=["MP"], writes=["yout"])
        S.op("dve", lambda e: e.scalar_tensor_tensor(out=MS, in0=NGE[:, 10:26], scalar=-1.0, in1=BB[:, 2064:2192].rearrange("p (q r) -> p q r", r=8)[:, :, 7], op0=ALU.mult, op1=ALU.subtract),
             reads=["NGE", "BB"], writes=["MS"])
        S.dma("sp", lambda e: e.dma_start(out=o_msT[:, :], in_=MS), reads=["MS"], writes=["yout"])
        S.op("dve", lambda e: e.tensor_tensor(out=XD.rearrange("p (j h) -> p j h", h=4), in0=CW.unsqueeze(2).to_broadcast([4, 9, 4]), in1=ident[0:4, 0:4].unsqueeze(1).to_broadcast([4, 9, 4]), op=ALU.mult),
             reads=["CW", "ident"], writes=["XD"])
        S.op("dve", lambda e: e.tensor_tensor(out=XDS.rearrange("p (j h) -> p j h", h=4), in0=CS.unsqueeze(2).to_broadcast([4, 16, 4]), in1=ident[0:4, 0:4].unsqueeze(1).to_broadcast([4, 16, 4]), op=ALU.mult),
             reads=["CS", "ident"], writes=["XDS"])
        b = nbank()
        mm(banks[b][:, 0:36], ones4[:, :], XD, True, True, ["ones4", "XD"], ["pb%d" % b])
        mm(banks[b][:, 64:128], ones4[:, :], XDS, True, True, ["ones4", "XDS"], ["pb%d" % b])
        S.op("dve", (lambda b=b: lambda e: e.tensor_copy(out=cwb[:], in_=banks[b][:, 0:36]))(), reads=["pb%d" % b], writes=["cwb"])
        S.op("dve", (lambda b=b: lambda e: e.tensor_copy(out=csb[:], in_=banks[b][:, 64:128]))(), reads=["pb%d" % b], writes=["csb"])
        GT = [(1024 + TILES[t][0], TILES[t][1]) for t in range(10)] + [(128 * t, 128) for t in range(8)]
        for ti, (a0, nn) in enumerate(GT):
            b = nbank()
            S.op("pe", (lambda b=b, a0=a0, nn=nn: lambda e: e.transpose(out=banks[b][:nn, 0:4], in_=EE[:, a0:a0 + nn], identity=ident[0:4, 0:4]))(),
                 reads=["EE", "ident"], writes=["pb%d" % b])
            S.op("dve", (lambda b=b, ti=ti, nn=nn: lambda e: e.tensor_copy(out=dectm[:nn, ti, :], in_=banks[b][:nn, 0:4]))(), reads=["pb%d" % b], writes=["dectm"])
            if ti < 10:
                b = nbank()
                S.op("pe", (lambda b=b, a0=a0, nn=nn: lambda e: e.transpose(out=banks[b][:nn, 0:4], in_=LF[:, a0:a0 + nn], identity=ident[0:4, 0:4]))(),
                     reads=["FL", "ident"], writes=["pb%d" % b])
                S.op("dve", (lambda b=b, ti=ti, nn=nn: lambda e: e.tensor_copy(out=floortm[:nn, ti, :], in_=banks[b][:nn, 0:4]))(), reads=["pb%d" % b], writes=["floortm"])
        prefetch_w("pk0", w_in0[:, 1024:1280], 16, 256)
        S.barrier()

        CST = ACC[:, 16368:20480].rearrange("p (h c e) -> p h c e", h=4, c=2)
        S.op("dve", lambda e: e.memset(ACC[:, 16368:20480], 0.0), writes=["cst0", "cst1", "cst2", "cst3"])
        kpre = ACC[:, 8192:9216].bitcast(BF16).rearrange("p (t d) -> p t d", t=8)
        vdpre = ACC[:, 9216:11272].bitcast(BF16).rearrange("p (t d) -> p t d", t=8)
        for h in range(4):
            wk, wkk = load_w(w_in0[:, 1024 + 256 * h:1280 + 256 * h], 16, 256, tag=("pk0" if h == 0 else None))
            for t in range(8):
                b = nbank()
                for kc in range(16):
                    mm(banks[b][:, 0:256], XPRE[:, kc, 128 * t:128 * (t + 1)], wk[:, kc, :], kc == 0, kc == 15, ["xpre", wkk], ["pb%d" % b])
                S.op("act", (lambda b=b, t=t: lambda e: e.activation(out=kpre[:, t, :], in_=banks[b][:, 0:256], func=AF.Copy, scale=0.0625))(),
                     reads=["pb%d" % b], writes=["kpre"])
            wv, wvk = load_w(w_in0[:, 2048 + 512 * h:2560 + 512 * h], 16, 512)
            for t in range(8):
                b = nbank()
                for kc in range(16):
                    mm(banks[b][:, :], XPRE[:, kc, 128 * t:128 * (t + 1)], wv[:, kc, :], kc == 0, kc == 15, ["xpre", wvk], ["pb%d" % b])
                S.op("act", (lambda b=b, t=t, h=h: lambda e: e.activation(out=vdpre[:, t, 0:512], in_=banks[b][:, :], func=AF.Copy, scale=dectm[:, 10 + t, h:h + 1]))(),
                     reads=["pb%d" % b, "dectm"], writes=["vdpre"])
                S.op("dve", (lambda t=t, h=h: lambda e: e.tensor_copy(out=vdpre[:, t, 512:513], in_=dectm[:, 10 + t, h:h + 1]))(),
                     reads=["dectm"], writes=["vdpre"])
            for dc in range(2):
                b, b2 = nbank(), nbank()
                for t in range(8):
                    mm(banks[b][:, :], kpre[:, t, dc * 128:(dc + 1) * 128], vdpre[:, t, 0:512], t == 0, t == 7, ["kpre", "vdpre"], ["pb%d" % b])
                for t in range(8):
                    mm(banks[b2][:, 0:1], kpre[:, t, dc * 128:(dc + 1) * 128], vdpre[:, t, 512:513], t == 0, t == 7, ["kpre", "vdpre"], ["pb%d" % b2])
                S.op("dve", (lambda b=b, h=h, dc=dc: lambda e: e.tensor_copy(out=CST[:, h, dc, 0:512], in_=banks[b][:, :]))(), reads=["pb%d" % b], writes=["cst%d" % h])
                S.op("dve", (lambda b2=b2, h=h, dc=dc: lambda e: e.tensor_copy(out=CST[:, h, dc, 512:513], in_=banks[b2][:, 0:1]))(), reads=["pb%d" % b2], writes=["cst%d" % h])
        prefetch_w("q0", w_in0[:, 0:256], 16, 256)
        S.barrier()

        o = 0
        def carve(ncols, dt=F32):
            nonlocal o
            v = ACC[:, o:o + ncols]
            o += ncols
            return v if dt == F32 else v.bitcast(dt)
        qT = carve(1168, BF16).rearrange("p (c n) -> p c n", c=2)
        kT = carve(1168, BF16).rearrange("p (c n) -> p c n", c=2)
        ktm = carve(1280, BF16).rearrange("p (t d) -> p t d", t=10)
        vdtm = carve(2570, BF16).rearrange("p (t d) -> p t d", t=10)
        og = carve(2560, BF16).rearrange("p (t d) -> p t d", t=10)
        CbL = [carve(514, BF16).rearrange("p (c e) -> p c e", c=2) for _ in range(2)]
        STbL = [carve(64, BF16) for _ in range(2)]
        hrowL = [carve(512) for _ in range(2)]
        NCQ = 2
        CqL = [carve(1024).rearrange("p (c e) -> p c e", c=2) for _ in range(NCQ)]
        CoL = [carve(1024).rearrange("p (c e) -> p c e", c=2) for _ in range(2)]
        CqbL = [carve(512, BF16).rearrange("p (c e) -> p c e", c=2) for _ in range(1)]
        k8s = carve(128, BF16)
        v8s = carve(257, BF16)
        kzL = [carve(128, BF16) for _ in range(2)]
        assert o <= 16368, o
        etmp = MISC[:, 2048:2304].bitcast(BF16)
        og8s = MISC[:, 2304:2560].bitcast(BF16)
        nraw = MISC[:, 2560:2688].rearrange("p (c n) -> p c n", c=2)
        nnew = MISC[:, 2688:2816].rearrange("p (c n) -> p c n", c=2)
        nsb = MISC[:, 2816:2880].bitcast(BF16).rearrange("p (c n) -> p c n", c=2)
        nl = hrowL[0][0:64, 0:256]
        qz = MISC[:, 0:2048].bitcast(BF16).rearrange("p (c q n) -> p c q n", c=2, q=16)
        qzf = MISC[:, 0:2048].bitcast(BF16).rearrange("p (c n) -> p c n", c=2)
        HGT = R2[:].bitcast(BF16).rearrange("p (c n) -> p c n", c=16)
        hsm = sb("hsm", [128, 32], F32)
        eps2 = hsm[:, 30:31]
        S.op("dve", lambda e: e.memset(eps2, EPS), writes=["eps2"])
        S.op("dve", lambda e: e.memset(qz, 0.0), writes=["qz"])
        hgfm = sb("hgfm", [128, 16], F32)
        S.dma("sp", lambda e: e.dma_start(out=hgfm[:], in_=headg[:, :]), writes=["hgfm"])
        S.dma("sp", lambda e: e.dma_start(out=nl, in_=sn.rearrange("q h d -> (q h) d")), writes=["hrow0"])
        bn_ = nbank()
        for dc in range(2):
            S.op("pe", (lambda dc=dc: lambda e: e.transpose(out=banks[bn_][:, dc * 64:(dc + 1) * 64], in_=nl[:, dc * 128:(dc + 1) * 128], identity=ident[0:64, 0:64]))(),
                 reads=["hrow0", "ident"], writes=["pb%d" % bn_])
        S.op("dve", lambda e: e.tensor_tensor(out=nraw, in0=banks[bn_][:, 0:128].rearrange("p (c n) -> p c n", c=2), in1=csb[:, :].unsqueeze(1).to_broadcast([128, 2, 64]), op=ALU.mult),
             reads=["pb%d" % bn_, "csb"], writes=["nraw"])
        S.op("act", lambda e: e.activation(out=nsb, in_=nraw, func=AF.Copy), reads=["nraw"], writes=["nsb"])
        BS, BA, BD, BC0, BC1, BN = 0, (1, 2), (3, 4), 5, 6, 7
        hctr = [0]

        def head_out(h, t, L, bA, bD, floor_ap, gate=None, gkey=None):
            p = hctr[0] % 2
            hctr[0] += 1
            o8 = 8 * p
            dn, rec, ssh, lvh, rsh = (hsm[:, o8 + i:o8 + i + 1] for i in range(5))
            hrow, hk = hrowL[p], "hrow%d" % p
            sx = "h%d" % p
            S.op("act", lambda e: e.activation(out=dn[:L], in_=banks[bD][:L, 0:1], func=AF.Abs), reads=["pb%d" % bD], writes=["dn" + sx])
            S.op("dve", lambda e: e.tensor_scalar(out=dn[:L], in0=dn[:L], scalar1=floor_ap, scalar2=None, op0=ALU.max),
                 reads=["dn" + sx, "floortm"], writes=["dn" + sx])
            S.op("dve", lambda e: e.reciprocal(out=rec[:L], in_=dn[:L]), reads=["dn" + sx], writes=["rec" + sx])
            S.op("act", lambda e: e.activation(out=hrow[:L], in_=banks[bA][:L, :], func=AF.Copy, scale=rec[:L]),
                 reads=["pb%d" % bA, "rec" + sx], writes=[hk])
            S.op("act", lambda e: e.activation(out=etmp[:L], in_=hrow[:L], func=AF.Square, accum_out=ssh[:L]), reads=[hk], writes=["ssh" + sx, "etmp"])
            S.op("act", lambda e: e.activation(out=lvh[:L], in_=ssh[:L], func=AF.Ln, bias=eps2[:L], scale=1.0 / 512), reads=["ssh" + sx, "eps2"], writes=["lvh" + sx])
            S.op("act", lambda e: e.activation(out=rsh[:L], in_=lvh[:L], func=AF.Exp, scale=-0.5), reads=["lvh" + sx], writes=["rsh" + sx])
            S.op("pool", lambda e: e.tensor_scalar(out=hrow[:L], in0=hrow[:L], scalar1=rsh[:L], scalar2=None, op0=ALU.mult), reads=[hk, "rsh" + sx], writes=[hk])
            g_ = og[:L, t, :] if gate is None else gate[:L]
            gk_ = ("og%d" % t) if gkey is None else gkey
            S.op("pool", lambda e: e.tensor_tensor(out=g_, in0=hrow[:L], in1=g_, op=ALU.mult),
                 reads=[hk, gk_], writes=[gk_])

        def proj_steps(h):
            steps = []
            holder = {}

            def ld(key, col, ncol):
                def f():
                    holder[key] = load_w(w_in0[:, col:col + ncol], 16, ncol, tag=("q0" if (h == 0 and key == "qT") else None))
                return f
            for (wcol, dstT, scl, key) in ((256 * h, qT, 1.0, "qT"), (1024 + 256 * h, kT, 0.0625, "kT")):
                steps.append(ld(key, wcol, 256))
                for dc in range(2):
                    for (c0, ncol) in TGS:
                        def f(dc=dc, c0=c0, ncol=ncol, dstT=dstT, scl=scl, key=key):
                            wq, wqk = holder[key]
                            b = nbank()
                            for kc in range(16):
                                mm(banks[b][:, :ncol], wq[:, kc, dc * 128:(dc + 1) * 128], XNT[:, kc, c0:c0 + ncol], kc == 0, kc == 15, ["xnT", wqk], ["pb%d" % b])
                            S.op("act", lambda e: e.activation(out=dstT[:, dc, c0:c0 + ncol], in_=banks[b][:, :ncol], func=AF.Copy, scale=scl),
                                 reads=["pb%d" % b], writes=[key])
                        steps.append(f)
                if key == "kT":
                    for t in range(10):
                        def f(t=t):
                            wq, wqk = holder["kT"]
                            col0, n = TILES[t]
                            b = nbank()
                            for kc in range(16):
                                mm(banks[b][:n, 0:256], XNT[:, kc, col0:col0 + n], wq[:, kc, :], kc == 0, kc == 15, ["xnT", wqk], ["pb%d" % b])
                            S.op("act", lambda e: e.activation(out=ktm[:n, t, :], in_=banks[b][:n, 0:256], func=AF.Copy, scale=0.0625),
                                 reads=["pb%d" % b], writes=["ktm%d" % t])
                        steps.append(f)
            steps.append(ld("v", 2048 + 512 * h, 512))
            for t in range(10):
                def f(t=t):
                    wv, wvk = holder["v"]
                    col0, n = TILES[t]
                    b = nbank()
                    for kc in range(16):
                        mm(banks[b][:n, :], XNT[:, kc, col0:col0 + n], wv[:, kc, :], kc == 0, kc == 15, ["xnT", wvk], ["pb%d" % b])
                    S.op("act", lambda e: e.activation(out=vdtm[:n, t, 0:512], in_=banks[b][:n, :], func=AF.Copy, scale=dectm[:n, t, h:h + 1]),
                         reads=["pb%d" % b, "dectm"], writes=["vdtm%d" % t])
                    S.op("dve", lambda e: e.tensor_copy(out=vdtm[:n, t, 512:513], in_=dectm[:n, t, h:h + 1]), reads=["dectm"], writes=["vdtm%d" % t])
                steps.append(f)
            steps.append(ld("o", 4096 + 512 * h, 512))
            for t in range(10):
                def f(t=t):
                    wo, wok = holder["o"]
                    col0, n = TILES[t]
                    b = nbank()
                    for kc in range(16):
                        mm(banks[b][:n, :], XNT[:, kc, col0:col0 + n], wo[:, kc, :], kc == 0, kc == 15, ["xnT", wok], ["pb%d" % b])
                    S.op("act", lambda e: e.activation(out=og[:n, t, :], in_=banks[b][:n, :], func=AF.Sigmoid), reads=["pb%d" % b], writes=["og%d" % t])
                steps.append(f)
            return steps

        def hgt_tiles(h, tiles, src8=None):
            for ti in tiles:
                col0_, n = TILES[ti]
                b = nbank()
                ptb = banks[b][:, 0:256].bitcast(BF16).rearrange("p (j n) -> p j n", j=4)
                for c in range(4):
                    src_ = og[:n, ti, c * 128:(c + 1) * 128] if src8 is None else src8[:n, c * 128:(c + 1) * 128]
                    S.op("pe", (lambda c=c, n=n, ptb=ptb, src_=src_: lambda e: e.transpose(out=ptb[:, c, :n], in_=src_, identity=identb[:n, :n]))(),
                         reads=[("og%d" % ti) if src8 is None else "og8s", "identb"], writes=["pb%d" % b])
                for c in range(4):
                    if ti % 2 == 0:
                        S.op("dve", (lambda c=c, col0_=col0_, n=n, ptb=ptb: lambda e: e.tensor_scalar(out=HGT[:, 4 * h + c, col0_:col0_ + n], in0=ptb[:, c, :n], scalar1=hgfm[:, 4 * h + c:4 * h + c + 1], scalar2=None, op0=ALU.mult))(),
                             reads=["pb%d" % b, "hgfm"], writes=["hgT_%d_%d_%d" % (h, c, ti)])
                    else:
                        S.op("act", (lambda c=c, col0_=col0_, n=n, ptb=ptb: lambda e: e.activation(out=HGT[:, 4 * h + c, col0_:col0_ + n], in_=ptb[:, c, :n], func=AF.Copy, scale=hgfm[:, 4 * h + c:4 * h + c + 1]))(),
                             reads=["pb%d" % b, "hgfm"], writes=["hgT_%d_%d_%d" % (h, c, ti)])

        for st_ in proj_steps(0):
            st_()
        for h in range(4):
            def load_cq(q, h=h):
                Cq = CqL[q % NCQ]
                S.dma("sp", (lambda: lambda e: e.dma_start(out=Cq, in_=sc[q, h].rearrange("(c p) e -> p c e", p=128)))(), writes=["Cq%d" % (q % NCQ)])
            for q in range(NCQ):
                load_cq(q)
            chunks = [(9, 16, 0)] + [(t, 128, t + 1) for t in range(8)]
            pend = None
            for ci, (t, L, j) in enumerate(chunks):
                col0 = TILES[t][0]
                pr = ci % 2
                Cb, STb, bA, bD = CbL[pr], STbL[pr], BA[pr], BD[pr]
                cbk, stk = "Cb%d" % pr, "STb%d" % pr
                cw = cwb[:, 4 * j + h:4 * j + h + 1]
                vk, kk = "vdtm%d" % t, "ktm%d" % t
                for dc, bC in ((0, BC0), (1, BC1)):
                    mm(banks[bC][:, :], ktm[:L, t, dc * 128:(dc + 1) * 128], vdtm[:L, t, 0:512], True, True, [kk, vk], ["pb%d" % bC])
                    mm(banks[BN][:, 32 + dc:33 + dc], ktm[:L, t, dc * 128:(dc + 1) * 128], vdtm[:L, t, 512:513], True, True, [kk, vk], ["pb%d" % BN])
                S.op("act", (lambda Cb=Cb, cw=cw, h=h: lambda e: e.activation(out=Cb, in_=CST[:, h], func=AF.Copy, scale=cw))(), reads=["cst%d" % h, "cwb"], writes=[cbk])
                if pend is not None:
                    head_out(*pend)
                    pend = None
                for dc, bC in ((0, BC0), (1, BC1)):
                    S.op("dve", (lambda bC=bC, h=h, dc=dc, cw=cw: lambda e: e.scalar_tensor_tensor(out=CST[:, h, dc, 0:512], in0=CST[:, h, dc, 0:512], scalar=cw, in1=banks[bC][:, :], op0=ALU.mult, op1=ALU.add))(),
                         reads=["pb%d" % bC, "cst%d" % h, "cwb"], writes=["cst%d" % h])
                S.op("dve", (lambda h=h, cw=cw: lambda e: e.scalar_tensor_tensor(out=CST[:, h, :, 512], in0=CST[:, h, :, 512], scalar=cw, in1=banks[BN][:, 32:34], op0=ALU.mult, op1=ALU.add))(),
                     reads=["pb%d" % BN, "cst%d" % h, "cwb"], writes=["cst%d" % h])
                for dc in range(2):
                    mm(banks[BS][:L, :L], kT[:, dc, col0:col0 + L], qT[:, dc, col0:col0 + L], dc == 0, dc == 1, ["kT", "qT"], ["pb%d" % BS])
                S.op("dve", (lambda STb=STb, L=L: lambda e: e.tensor_tensor(out=STb[:L, :L], in0=banks[BS][:L, :L], in1=maskU[:L, :L], op=ALU.mult))(),
                     reads=["pb%d" % BS, "maskU"], writes=[stk])
                mm(banks[bA][:L, :], STb[:L, :L], vdtm[:L, t, 0:512], True, False, [stk, vk], ["pb%d" % bA])
                for dc in range(2):
                    mm(banks[bA][:L, :], qT[:, dc, col0:col0 + L], Cb[:, dc, 0:512], False, dc == 1, ["qT", cbk], ["pb%d" % bA])
                mm(banks[bD][:L, 0:1], STb[:L, :L], vdtm[:L, t, 512:513], True, False, [stk, vk], ["pb%d" % bD])
                for dc in range(2):
                    mm(banks[bD][:L, 0:1], qT[:, dc, col0:col0 + L], Cb[:, dc, 512:513], False, dc == 1, ["qT", cbk], ["pb%d" % bD])
                pend = (h, t, L, bA, bD, floortm[:L, t, h:h + 1])
            head_out(*pend)
            S.dma("sp", (lambda h=h: lambda e: e.dma_start(out=o_cp[h].rearrange("(c p) e -> p c e", p=128), in_=CST[:, h, :, 0:512]))(), reads=["cst%d" % h], writes=["yout"])
            S.dma("sp", (lambda h=h: lambda e: e.dma_start(out=o_np[h].rearrange("(c p) -> p c", p=128), in_=CST[:, h, :, 512]))(), reads=["cst%d" % h], writes=["yout"])
            hgt_tiles(h, [0, 1, 2, 3, 4, 5, 6, 7, 9])
            t = 8
            col0 = TILES[8][0]
            S.op("dve", lambda e: e.tensor_copy(out=qzf[:, :, 0:2040].rearrange("p c (q s) -> p c q s", s=136)[:, :, :, 0:8], in_=qT[:, :, col0:col0 + 120].rearrange("p c (q r) -> p c q r", r=8)),
                 reads=["qT", "qz"], writes=["qz"])
            S.op("dve", lambda e: e.tensor_copy(out=qzf[:, :, 2040:2048], in_=qT[:, :, col0 + 120:col0 + 128]), reads=["qT", "qz"], writes=["qz"])
            S.op("act", lambda e: e.activation(out=k8s, in_=ktm[:, 8, :], func=AF.Copy), reads=["ktm8"], writes=["k8s"])
            S.op("act", lambda e: e.activation(out=v8s[:, 0:513], in_=vdtm[:, 8, 0:513], func=AF.Copy), reads=["vdtm8"], writes=["v8s"])
            S.op("act", lambda e: e.activation(out=og8s, in_=og[:, 8, :], func=AF.Copy), reads=["og8"], writes=["og8s"])
            bA, bD = BA[0], BD[0]
            STb = STbL[0]
            for dc in range(2):
                mm(banks[BS][:, 0:128], kT[:, dc, col0:col0 + 128], qT[:, dc, col0:col0 + 128], dc == 0, dc == 1, ["kT", "qT"], ["pb%d" % BS])
            S.op("dve", lambda e: e.tensor_tensor(out=STb[:, :], in0=banks[BS][:, 0:128], in1=maskS[:, :], op=ALU.mult),
                 reads=["pb%d" % BS, "maskS"], writes=["STb0"])
            mm(banks[bA][:, :], STb[:, :], v8s[:, 0:512], True, False, ["STb0", "v8s"], ["pb%d" % bA])
            mm(banks[bD][:, 0:1], STb[:, :], v8s[:, 512:513], True, False, ["STb0", "v8s"], ["pb%d" % bD])

            def qstep(q, h=h, bA=bA, bD=bD):
                Cq, cqk = CqL[q % NCQ], "Cq%d" % (q % NCQ)
                Cqb, cqbk = CqbL[0], "Cqb0"
                kz, kzk = kzL[q % 2], "kz%d" % (q % 2)
                cs = csb[:, 4 * q + h:4 * q + h + 1]
                S.op("act", lambda e: e.activation(out=Cqb, in_=Cq, func=AF.Copy, scale=cs), reads=[cqk, "csb"], writes=[cqbk])
                S.op("dve", lambda e: e.tensor_scalar(out=kz, in0=k8s, scalar1=ind[:, q:q + 1], scalar2=None, op0=ALU.mult),
                     reads=["k8s", "ind"], writes=[kzk])
                for dc in range(2):
                    last = (q == 15 and dc == 1)
                    mm(banks[bA][:, :], qz[:, dc, q, :], Cqb[:, dc, :], False, last, ["qz", cqbk], ["pb%d" % bA])
                    mm(banks[bD][:, 0:1], qz[:, dc, q, :], nsb[:, dc, 4 * q + h:4 * q + h + 1], False, last, ["qz", "nsb"], ["pb%d" % bD])
                for dc, bC in ((0, BC0), (1, BC1)):
                    mm(banks[bC][:, :], kz[:, dc * 128:(dc + 1) * 128], v8s[:, 0:512], True, True, [kzk, "v8s"], ["pb%d" % bC])
                    mm(banks[BN][:, dc * 16 + q:dc * 16 + q + 1], kz[:, dc * 128:(dc + 1) * 128], v8s[:, 512:513], True, True, [kzk, "v8s"], ["pb%d" % BN])
                Co, cok = CoL[q % 2], "Co%d" % (q % 2)
                for dc, bC in ((0, BC0), (1, BC1)):
                    S.op("dve", (lambda bC=bC, dc=dc: lambda e: e.scalar_tensor_tensor(out=Co[:, dc, :], in0=Cq[:, dc, :], scalar=cs, in1=banks[bC][:, :], op0=ALU.mult, op1=ALU.add))(),
                         reads=["pb%d" % bC, cqk, "csb"], writes=[cok])
                S.dma("sp", lambda e: e.dma_start(out=o_cs[q, h].rearrange("(c p) e -> p c e", p=128), in_=Co), reads=[cok], writes=["yout"])
                if q + NCQ < 16:
                    load_cq(q + NCQ, h)

            qsteps = [(lambda q=q: qstep(q)) for q in range(16)]
            psteps = proj_steps(h + 1) if h < 3 else []
            bank_pool[0] = [0, 2, 4]
            np_, nq_ = len(psteps), len(qsteps)
            pi = 0
            for qi in range(nq_):
                qsteps[qi]()
                tgt = (np_ * (qi + 1)) // nq_
                while pi < tgt:
                    psteps[pi]()
                    pi += 1
            while pi < np_:
                psteps[pi]()
                pi += 1
            bank_pool[0] = list(range(8))
            S.op("dve", (lambda h=h: lambda e: e.tensor_tensor(out=nnew.rearrange("p c (q f) -> p c q f", f=4)[:, :, :, h], in0=nraw.rearrange("p c (q f) -> p c q f", f=4)[:, :, :, h], in1=banks[BN][:, 0:32].rearrange("p (c q) -> p c q", c=2), op=ALU.add))(),
                 reads=["pb%d" % BN, "nraw"], writes=["nnew"])
            head_out(h, 8, 128, bA, bD, floortm[:, 8, h:h + 1], gate=og8s, gkey="og8s")
            hgt_tiles(h, [8], src8=og8s)
        nout = hrowL[1][0:64, 0:256]
        bn2 = nbank()
        for dc in range(2):
            S.op("pe", (lambda dc=dc: lambda e: e.transpose(out=banks[bn2][0:64, dc * 128:(dc + 1) * 128], in_=nnew[:, dc, :], identity=ident[:, :]))(),
                 reads=["nnew", "ident"], writes=["pb%d" % bn2])
        S.op("dve", lambda e: e.tensor_copy(out=nout, in_=banks[bn2][0:64, 0:256]), reads=["pb%d" % bn2], writes=["hrow1"])
        S.dma("sp", lambda e: e.dma_start(out=o_ns.rearrange("q h d -> (q h) d"), in_=nout), reads=["hrow1"], writes=["yout"])
        prefetch_w("wo0", w_out0[:, 0:512], 16, 512)
        S.barrier()
        S.dma("sp", lambda e: e.dma_start(out=growM, in_=gpost[0:1, :].partition_broadcast(128)[:, 0, :]), writes=["growM"])
        proj_to_acc(HGT, "hgT", w_out0, range(10), tag="wo0")
        S.barrier()
        mode[0] = 'B'
        prefetch_w("up0", w_up[0, :, 0:512], 16, 512)
        pipeline3([boundary_stage(t, TILES[t][1], xin[128 * t:128 * t + TILES[t][1], :], hscr[128 * t:128 * t + TILES[t][1], :], None, 0, 1, XNT, TILES[t][0]) for t in range(10)])
        S.barrier()
        ffn(0, range(10))
        S.barrier()
        load_grow(1)
        prefetch_w("pin", pw_in[:, 1536:2048], 16, 512)
        pipeline3([boundary_stage(t, TILES[t][1], hscr[128 * t:128 * t + TILES[t][1], :], hscr[128 * t:128 * t + TILES[t][1], :], None, 1, 2, XNT, TILES[t][0]) for t in range(10)])
        S.barrier()

        UTgL = [ACC[:, 0:5632].rearrange("p (c n) -> p c n", c=4), ACC[:, 5632:11264].rearrange("p (c n) -> p c n", c=4)]
        SPT = [ACC[:, 11264:13312], ACC[:, 13312:15360]]
        T1 = ACC[:, 15360:16400]
        T2 = ACC[:, 16400:17440]
        GA = ACC[:, 17440:17952]
        XSO = ACC[:, 17952:18464]
        TMO = ACC[:, 18464:18976]
        PLT = R2[:].bitcast(BF16).rearrange("p (c n) -> p c n", c=16)
        for half in range(2):
            S.dma("sp", (lambda half=half: lambda e: e.dma_start(out=SPT[half][:120], in_=spool[8 * half:8 * half + 8].rearrange("q r d -> (q r) d")))(), writes=["spt"])
        S.dma("sp", lambda e: e.dma_start(out=o_pools[:, 0:7, :], in_=spool[:, 8:15, :]), writes=["yout"])
        S.op("dve", lambda e: e.memset(PLT[:, :, 0:16], 0.0), writes=["plt"])
        S.op("dve", lambda e: e.memset(T1, 0.0), writes=["T1"])
        S.op("dve", lambda e: e.memset(T2, 0.0), writes=["T2"])
        for it_, cg in enumerate((3, 2, 1, 0)):
            w = 2 ** (cg + 1)
            UTg = UTgL[it_ % 2]
            up_ = "UT%d_" % (it_ % 2)
            wv, wk_ = load_w(pw_in[:, cg * 512:(cg + 1) * 512], 16, 512, tag=("pin" if cg == 3 else None))
            for half in range(2):
                b = nbank()
                pt = banks[b][:].rearrange("p (j n) -> p j n", j=4)
                for j in range(4):
                    c = cg * 4 + j
                    S.op("pe", (lambda c=c, j=j, pt=pt, half=half: lambda e: e.transpose(out=pt[:, j, :120], in_=SPT[half][:120, c * 128:(c + 1) * 128], identity=ident[:120, :120]))(),
                         reads=["spt", "ident"], writes=["pb%d" % b])
                for j in range(4):
                    dst = UTg[:, j, 1040 + 184 * half:1040 + 184 * half + 184].rearrange("p (q r) -> p q r", r=23)[:, :, 0:15]
                    S.op("dve", (lambda j=j, pt=pt, dst=dst: lambda e: e.tensor_copy(out=dst, in_=pt[:, j, :120].rearrange("p (q r) -> p q r", r=15)))(),
                         reads=["pb%d" % b], writes=[up_ + str(j)])
            for oc in range(4):
                for (c0, ncol) in TGS:
                    b = nbank()
                    for kc in range(16):
                        mm(banks[b][:, :ncol], wv[:, kc, oc * 128:(oc + 1) * 128], XNT[:, kc, c0:c0 + ncol], kc == 0, kc == 15, ["xnT", wk_], ["pb%d" % b])
                    if c0 < 1024:
                        S.op("act", (lambda b=b, oc=oc, c0=c0, ncol=ncol, UTg=UTg: lambda e: e.activation(out=UTg[:, oc, c0:c0 + ncol], in_=banks[b][:, :ncol], func=AF.Copy))(),
                             reads=["pb%d" % b], writes=[up_ + str(oc)])
                    else:
                        S.op("act", (lambda b=b, oc=oc, UTg=UTg: lambda e: e.activation(out=UTg[:, oc, 1024:1040], in_=banks[b][:, 0:16], func=AF.Copy))(),
                             reads=["pb%d" % b], writes=[up_ + str(oc)])
                        dst = UTg[:, oc, 1040:1408].rearrange("p (q r) -> p q r", r=23)[:, :, 15:23]
                        S.op("dve", (lambda b=b, dst=dst: lambda e: e.tensor_copy(out=dst, in_=banks[b][:, 16:144].rearrange("p (q r) -> p q r", r=8)))(),
                             reads=["pb%d" % b], writes=[up_ + str(oc)])
            b = nbank()
            pt = banks[b][:].rearrange("p (j n) -> p j n", j=4)
            for j in range(4):
                S.op("pe", (lambda j=j, pt=pt, UTg=UTg: lambda e: e.transpose(out=pt[:16, j, :], in_=UTg[:, j, 1024:1040], identity=ident[:, :]))(),
                     reads=[up_ + str(j), "ident"], writes=["pb%d" % b])
            S.op("dve", (lambda cg=cg, pt=pt: lambda e: e.tensor_copy(out=XSO[:16, :], in_=pt[:16].rearrange("p j n -> p (j n)")))(),
                 reads=["pb%d" % b], writes=["xso"])
            S.dma("sp", (lambda cg=cg: lambda e: e.dma_start(out=o_poolp[:, cg * 512:(cg + 1) * 512], in_=XSO[:16, :]))(), reads=["xso"], writes=["yout"])
            b = nbank()
            pt = banks[b][:].rearrange("p (j n) -> p j n", j=4)
            for j in range(4):
                sv = UTg[:, j, 1040:1408].rearrange("p (q r) -> p q r", r=23)[:, :, 15:23]
                S.op("dve", (lambda j=j, sv=sv: lambda e: e.tensor_copy(out=GA[:, j * 128:(j + 1) * 128].rearrange("p (q r) -> p q r", r=8), in_=sv))(),
                     reads=[up_ + str(j), "GA"], writes=["GA"])
                S.op("pe", (lambda j=j, pt=pt: lambda e: e.transpose(out=pt[:, j, :], in_=GA[:, j * 128:(j + 1) * 128], identity=ident[:, :]))(),
                     reads=["GA", "ident"], writes=["pb%d" % b])
            S.op("dve", (lambda cg=cg, pt=pt: lambda e: e.tensor_copy(out=TMO[:, :], in_=pt[:].rearrange("p j n -> p (j n)")))(),
                 reads=["pb%d" % b], writes=["tmo"])
            for q in range(16):
                S.dma("sp", (lambda q=q, cg=cg: lambda e: e.dma_start(out=o_pools[q, 7:15, cg * 512:(cg + 1) * 512], in_=TMO[8 * q:8 * q + 8, :]))(), reads=["tmo"], writes=["yout"])
            for j in range(4):
                c = 4 * cg + j
                for (lo, hi, is_p) in ((0, 1040, True), (1040, 1408, False)):
                    A = UTg[:, j, lo:hi]
                    nn = hi - lo
                    cur, sh, bi = A, 1, 0
                    bufs = [T1, T2]
                    while sh < w:
                        dstb = bufs[bi]
                        S.op("dve", (lambda dstb=dstb, cur=cur, sh=sh, nn=nn: lambda e: e.tensor_tensor(out=dstb[:, sh:nn], in0=cur[:, sh:nn], in1=cur[:, 0:nn - sh], op=ALU.add))(),
                             reads=[up_ + str(j), "T1", "T2"], writes=["T1" if bi == 0 else "T2"])
                        cur = dstb
                        sh *= 2
                        bi ^= 1
                    if is_p:
                        src_s, src_u, dst = cur[:, 16:1040], A[:, 16:1040], PLT[:, c, 16:1040]
                    else:
                        src_s = cur[:, 0:368].rearrange("p (q r) -> p q r", r=23)[:, :, 15:23]
                        src_u = A.rearrange("p (q r) -> p q r", r=23)[:, :, 15:23]
                        dst = PLT[:, c, 1040:1168].rearrange("p (q r) -> p q r", r=8)
                    S.op("dve", (lambda src_s=src_s, src_u=src_u, dst=dst, w=w: lambda e: e.scalar_tensor_tensor(out=dst, in0=src_s, scalar=1.0 / w, in1=src_u, op0=ALU.mult, op1=ALU.subtract))(),
                         reads=[up_ + str(j), "T1", "T2"], writes=["plt"])
        S.barrier()
        for g in range(4):
            wgm, wgmk = load_w(pw_group[g], 4, 512)
            for oc in range(4):
                c = 4 * g + oc
                for (c0, ncol) in TGS:
                    b = nbank()
                    for kc in range(4):
                        mm(banks[b][:, :ncol], wgm[:, kc, oc * 128:(oc + 1) * 128], PLT[:, 4 * g + kc, c0:c0 + ncol], kc == 0, kc == 3, ["plt", wgmk], ["pb%d" % b])
                    S.op("act", (lambda b=b, c=c, c0=c0, ncol=ncol: lambda e: e.activation(out=XNT[:, c, c0:c0 + ncol], in_=banks[b][:, :ncol], func=AF.Copy, scale=pscale[:, c:c + 1]))(),
                         reads=["pb%d" % b, "pscale"], writes=["mixT"])
        prefetch_w("pwo", pw_out[:, 0:512], 16, 512)
        S.barrier()
        S.dma("sp", lambda e: e.dma_start(out=growM, in_=gpost[2:3, :].partition_broadcast(128)[:, 0, :]), writes=["growM"])
        proj_to_acc(XNT, "mixT", pw_out, range(9), tag="pwo")
        S.barrier()
        prefetch_w("up1", w_up[1, :, 0:512], 16, 512)
        pipeline3([boundary_stage(t, TILES[t][1], hscr[128 * t:128 * t + TILES[t][1], :], hscr[128 * t:128 * t + TILES[t][1], :], None, 2, 3, XNT, TILES[t][0]) for t in range(9)])
        S.barrier()
        ffn(1, range(9))
        S.barrier()
        load_grow(3)
        pipeline3([boundary_stage(t, TILES[t][1], hscr[128 * t:128 * t + TILES[t][1], :], None, y[128 * t:128 * t + TILES[t][1], :], 3, None, None, TILES[t][0]) for t in range(9)])
        S.barrier()
        S.op("sp", lambda e: None, reads=["yout"])
        with nc.allow_non_contiguous_dma(reason='tiny state vectors'), nc.allow_low_precision(reason='bf16 matmul operands by design'):
            S.emit()
    return nc


_NC = None


def _prep(inputs):
    f = np.float32
    x_prompt = np.asarray(inputs["x_prompt"], f)
    x_sample = np.asarray(inputs["x_sample"], f)
    meta = np.asarray(inputs["meta_tokens"], f)
    sc = np.asarray(inputs["state_mlstm_c"], f)[0]
    sn = np.asarray(inputs["state_mlstm_n"], f)[0]
    sm = np.asarray(inputs["state_mlstm_m"], f)[0]
    spool = np.asarray(inputs["state_pool"], f)[0]
    nmp, nmq = np.asarray(inputs["norm_mix_pre"], f), np.asarray(inputs["norm_mix_post"], f)
    nfp, nfq = np.asarray(inputs["norm_ffn_pre"], f), np.asarray(inputs["norm_ffn_post"], f)
    gpre = np.stack([nmp[0], nfp[0], nmp[1], nfp[1]], 0)
    gpre_fm = np.ascontiguousarray(gpre.reshape(4, 16, 128).transpose(2, 0, 1))
    gpost = np.ascontiguousarray(np.stack([nmq[0], nfq[0], nmq[1], nfq[1]], 0))
    p = np.arange(128)
    ident = np.eye(128, dtype=f)
    maskU = (p[None, :] >= p[:, None]).astype(f)
    maskS = ((p[None, :] >= p[:, None]) & ((p[None, :] // 8) == (p[:, None] // 8))).astype(f)
    indm = ((p[:, None] // 8) == np.arange(16)[None, :]).astype(f)
    shared = {
        "gpre_fm": gpre_fm, "gpost": gpost,
        "w_in0": np.ascontiguousarray(np.asarray(inputs["mlstm_w_in"], f)[0]),
        "gb_i": np.ascontiguousarray(np.asarray(inputs["mlstm_b_i"], f)[0].reshape(4, 1)),
        "gb_f": np.ascontiguousarray(np.asarray(inputs["mlstm_b_f"], f)[0].reshape(4, 1)),
        "headg": np.ascontiguousarray(np.asarray(inputs["mlstm_head_norm"], f)[0].reshape(16, 128).T),
        "w_out0": np.ascontiguousarray(np.asarray(inputs["mlstm_w_out"], f)[0]),
        "pw_in": np.ascontiguousarray(np.asarray(inputs["pool_w_in"], f)[0]),
        "pw_group": np.ascontiguousarray(np.asarray(inputs["pool_w_group"], f)[0]),
        "pscale_fm": np.ascontiguousarray(np.asarray(inputs["pool_scale"], f)[0].reshape(16, 128).T),
        "pw_out": np.ascontiguousarray(np.asarray(inputs["pool_w_out"], f)[0]),
        "w_up": np.ascontiguousarray(np.asarray(inputs["ffn_w_up"], f)),
        "w_down": np.ascontiguousarray(np.asarray(inputs["ffn_w_down"], f)),
        "c_ident": ident, "c_maskU": maskU, "c_maskS": maskS, "c_ind": indm,
    }
    maps = []
    for c in range(8):
        b, half = c // 2, c % 2
        own = x_prompt[b, 1024 * half:1024 * half + 1024]
        halo = meta if half == 0 else x_prompt[b, 1008:1024]
        samp = x_sample[16 * c:16 * c + 16].reshape(128, D)
        xin = np.ascontiguousarray(np.concatenate([own, samp, halo], 0))
        xpre = np.zeros((NPRE, D), f) if half == 0 else np.ascontiguousarray(np.concatenate([meta, x_prompt[b, 0:1008]], 0))
        m = dict(shared)
        m.update({
            "xin": xin, "xpre": xpre,
            "sc": np.ascontiguousarray(sc[16 * c:16 * c + 16]),
            "sn": np.ascontiguousarray(sn[16 * c:16 * c + 16]),
            "smT": np.ascontiguousarray(sm[16 * c:16 * c + 16].T),
            "spool": np.ascontiguousarray(spool[16 * c:16 * c + 16]),
        })
        maps.append(m)
    return maps


def kernel(**inputs):
    global _NC
    maps = _prep(inputs)
    if _NC is None:
        _NC = build_nc()
    res = run_bass_kernel_spmd(_NC, maps, core_ids=list(range(8)))
    R = res.results
    f = np.float32
    y_prompt = np.zeros((4, 2048, D), f)
    y_sample = np.zeros((128, 8, D), f)
    c_p = np.zeros((1, 4, 4, 256, 512), f)
    n_p = np.zeros((1, 4, 4, 256), f)
    m_p = np.zeros((1, 4, 4), f)
    pool_p = np.zeros((1, 4, 15, D), f)
    c_s = np.zeros((1, 128, 4, 256, 512), f)
    n_s = np.zeros((1, 128, 4, 256), f)
    m_s = np.zeros((1, 128, 4), f)
    pool_s = np.zeros((1, 128, 15, D), f)
    for c in range(8):
        b, half = c // 2, c % 2
        r = R[c]
        y_prompt[b, 1024 * half:1024 * half + 1024] = r["y"][0:1024]
        y_sample[16 * c:16 * c + 16] = r["y"][1024:1152].reshape(16, 8, D)
        c_s[0, 16 * c:16 * c + 16] = r["o_cs"]
        n_s[0, 16 * c:16 * c + 16] = r["o_ns"]
        m_s[0, 16 * c:16 * c + 16] = r["o_msT"].T
        pool_s[0, 16 * c:16 * c + 16] = r["o_pools"]
        if half == 1:
            c_p[0, b] = r["o_cp"]
            n_p[0, b] = r["o_np"]
            m_p[0, b] = r["o_mp"][:, 0]
            pool_p[0, b] = r["o_poolp"][1:16]
    return (y_prompt, y_sample, c_p, n_p, m_p, pool_p, c_s, n_s, m_s, pool_s)
```

```python
import numpy as np
from contextlib import ExitStack
import concourse.bass as bass
import concourse.mybir as mybir
from concourse.bass_utils import run_bass_kernel_spmd

F32, BF16 = mybir.dt.float32, mybir.dt.bfloat16
AF = mybir.ActivationFunctionType
ALU = mybir.AluOpType

D = 2048
NT = 1168
NPRE = 1024
EPS = 1e-6
DFF = 8192
TILES = [(16 + 128 * t, 128) for t in range(8)] + [(1040, 128), (0, 16)]
TGS = [(0, 512), (512, 512), (1024, 144)]
ENGS = ("pe", "act", "dve", "pool", "sp")


class _Op:
    __slots__ = ("eng", "fn", "deps", "is_dma", "sig", "sem", "semval", "idx")
    _ctr = [0]

    def __init__(self, eng, fn, is_dma):
        self.eng, self.fn, self.is_dma = eng, fn, is_dma
        self.deps, self.sig, self.sem, self.semval = [], False, None, None
        _Op._ctr[0] += 1
        self.idx = _Op._ctr[0]


class Sched:
    def __init__(self, nc, n_dma_sems=60):
        self.nc = nc
        self.ops = {e: [] for e in ENGS}
        self.last_w, self.readers = {}, {}
        self.n_dma_sems = n_dma_sems
        self.since_barrier = []

    def _add(self, eng, fn, reads, writes, is_dma):
        op = _Op(eng, fn, is_dma)
        deps = set()
        for k in reads:
            w = self.last_w.get(k)
            if w is not None:
                deps.add(w)
        for k in writes:
            w = self.last_w.get(k)
            if w is not None:
                deps.add(w)
            for r in self.readers.get(k, ()):
                deps.add(r)
        best = {}
        for d in deps:
            if d.eng == eng and not d.is_dma and not is_dma and eng == "pe":
                continue
            if d.is_dma or d.fn is None:
                op.deps.append(d)
            else:
                b_ = best.get(d.eng)
                if b_ is None or d.idx > b_.idx:
                    best[d.eng] = d
        op.deps.extend(best.values())
        for k in reads:
            lst = self.readers.setdefault(k, [])
            if not is_dma:
                lst[:] = [r for r in lst if r.is_dma or r.eng != eng]
            lst.append(op)
        for k in writes:
            self.last_w[k] = op
            self.readers[k] = []
        self.ops[eng].append(op)
        if is_dma:
            self.since_barrier.append(op)
        return op

    def op(self, eng, fn, reads=(), writes=()):
        return self._add(eng, fn, tuple(reads), tuple(writes), False)

    def dma(self, eng, fn, reads=(), writes=()):
        return self._add(eng, fn, tuple(reads), tuple(writes), True)

    def barrier(self):
        lasts = []
        for e in ENGS:
            for o in reversed(self.ops[e]):
                if not o.is_dma and o.fn is not None:
                    lasts.append(o)
                    break
        lasts += self.since_barrier
        self.since_barrier = []
        for e in ENGS:
            o = _Op(e, None, False)
            o.deps = [d for d in lasts]
            self.ops[e].append(o)
        self.last_w, self.readers = {}, {}

    def emit(self):
        nc = self.nc
        for e in ENGS:
            for op in self.ops[e]:
                for d in op.deps:
                    d.sig = True
        with ExitStack() as st:
            esem = {e: st.enter_context(nc.semaphore("prog_" + e)) for e in ENGS}
            dsems = [st.enter_context(nc.semaphore("dma%d" % i)) for i in range(self.n_dma_sems)]
            for e in ENGS:
                cnt = 0
                for op in self.ops[e]:
                    if op.is_dma or op.fn is None:
                        continue
                    if op.sig:
                        cnt += 1
                        op.sem, op.semval = esem[e], cnt
            dcount = [0] * self.n_dma_sems
            dlast = [None] * self.n_dma_sems
            dma_prev = {}
            dma_engs = [e for e in ENGS if any(o.is_dma for o in self.ops[e])]
            per = self.n_dma_sems // max(1, len(dma_engs))
            for ei, e in enumerate(dma_engs):
                nxt = 0
                for op in self.ops[e]:
                    if not op.is_dma:
                        continue
                    s = ei * per + nxt
                    nxt = (nxt + 1) % per
                    if dlast[s] is not None:
                        dma_prev[op] = dlast[s]
                    dcount[s] += 16
                    dlast[s] = op
                    op.sem, op.semval, op.sig = dsems[s], dcount[s], True
            block = st.enter_context(nc.Block())
            engobj = {"pe": "tensor", "act": "scalar", "dve": "vector", "pool": "gpsimd", "sp": "sync"}

            def make(e):
                def body(eng):
                    waited = {}
                    for op in self.ops[e]:
                        deps = list(op.deps)
                        if op in dma_prev:
                            deps.append(dma_prev[op])
                        need = {}
                        for d in deps:
                            if d.sem is None:
                                continue
                            key = id(d.sem)
                            if need.get(key, (None, 0))[1] < d.semval:
                                need[key] = (d.sem, d.semval)
                        for key, (sem, val) in need.items():
                            if waited.get(key, 0) >= val:
                                continue
                            eng.wait_ge(sem, val)
                            waited[key] = val
                        if op.fn is None:
                            continue
                        ins = op.fn(eng)
                        if ins is None:
                            continue
                        if op.sig:
                            ins.then_inc(op.sem, 16 if op.is_dma else 1)
                return body

            for e in ENGS:
                if self.ops[e]:
                    getattr(block, engobj[e])(make(e))


def build_nc(stage=99):
    nc = bass.Bass("TRN2", target_bir_lowering=False)

    def din(name, shape):
        return nc.dram_tensor(name, list(shape), F32, kind="ExternalInput").ap()

    def dout(name, shape):
        return nc.dram_tensor(name, list(shape), F32, kind="ExternalOutput").ap()

    xin = din("xin", [NT, D])
    xpre = din("xpre", [NPRE, D])
    sc = din("sc", [16, 4, 256, 512])
    sn = din("sn", [16, 4, 256])
    smT = din("smT", [4, 16])
    spool = din("spool", [16, 15, D])
    gpre_fm = din("gpre_fm", [128, 4, 16])
    gpost = din("gpost", [4, D])
    w_in0 = din("w_in0", [D, 6152])
    gb_i = din("gb_i", [4, 1])
    gb_f = din("gb_f", [4, 1])
    headg = din("headg", [128, 16])
    w_out0 = din("w_out0", [D, D])
    pw_in = din("pw_in", [D, D])
    pw_group = din("pw_group", [4, 512, 512])
    pscale_fm = din("pscale_fm", [128, 16])
    pw_out = din("pw_out", [D, D])
    w_up = din("w_up", [2, D, DFF])
    w_down = din("w_down", [2, DFF, D])
    c_ident = din("c_ident", [128, 128])
    c_maskU = din("c_maskU", [128, 128])
    c_maskS = din("c_maskS", [128, 128])
    c_ind = din("c_ind", [128, 16])

    y = dout("y", [1152, D])
    o_cp = dout("o_cp", [4, 256, 512])
    o_np = dout("o_np", [4, 256])
    o_mp = dout("o_mp", [4, 1])
    o_poolp = dout("o_poolp", [16, D])
    o_cs = dout("o_cs", [16, 4, 256, 512])
    o_ns = dout("o_ns", [16, 4, 256])
    o_msT = dout("o_msT", [4, 16])
    o_pools = dout("o_pools", [16, 15, D])
    hscr = nc.dram_tensor("hscr", [NT, D], F32, kind="Internal").ap()

    with ExitStack() as st:
        def sb(name, shape, dt):
            return st.enter_context(nc.sbuf_tensor(name, list(shape), dt))

        ACC = sb("ACC", [128, 20480], F32)
        XNT = sb("XNT", [128, 16, NT], BF16)
        R2 = sb("R2", [128, 9344], F32)
        WB = [sb("WB%d" % i, [128, 8192], BF16) for i in range(2)]
        MISC = sb("MISC", [128, 3072], F32)
        ident = sb("ident", [128, 128], F32)
        identb = sb("identb", [128, 128], BF16)
        maskU = sb("maskU", [128, 128], BF16)
        maskS = sb("maskS", [128, 128], BF16)
        ind = sb("ind", [128, 16], F32)
        gpre = sb("gpre", [128, 4, 16], F32)
        pscale = sb("pscale", [128, 16], F32)
        small = sb("small", [128, 64], F32)
        banks = [st.enter_context(nc.psum_tensor("bank%d" % i, [128, 512], F32)) for i in range(8)]

        acc = ACC[:].rearrange("p (t d) -> p t d", t=10)
        S = Sched(nc)
        bank_ctr = [0]

        bank_excl = set()

        bank_pool = [list(range(8))]

        def nbank():
            pool_ = bank_pool[0]
            i = pool_[bank_ctr[0] % len(pool_)]
            bank_ctr[0] += 1
            return i

        wb_ctr = [0]

        pending_pref = {}

        def prefetch_w(tag, src2d, R, C):
            dst = WB[0][:, 0:R * C].rearrange("p (k c) -> p k c", k=R)
            S.dma("pool", lambda e: e.dma_start(out=dst, in_=src2d.rearrange("(k p) c -> p k c", p=128)), writes=["wb0"])
            pending_pref[tag] = (dst, "wb0")

        def load_w(src2d, R, C, tag=None):
            if tag is not None and tag in pending_pref:
                wb_ctr[0] = 1
                return pending_pref.pop(tag)
            i = wb_ctr[0] % 2
            wb_ctr[0] += 1
            dst = WB[i][:, 0:R * C].rearrange("p (k c) -> p k c", k=R)
            S.dma("pool", lambda e: e.dma_start(out=dst, in_=src2d.rearrange("(k p) c -> p k c", p=128)),
                  writes=["wb%d" % i])
            return dst, "wb%d" % i

        S.dma("sp", lambda e: e.dma_start(out=ident[:], in_=c_ident[:, :]), writes=["ident"])
        S.dma("pool", lambda e: e.dma_start(out=identb[:], in_=c_ident[:, :]), writes=["identb"])
        S.dma("pool", lambda e: e.dma_start(out=maskU[:], in_=c_maskU[:, :]), writes=["maskU"])
        S.dma("pool", lambda e: e.dma_start(out=maskS[:], in_=c_maskS[:, :]), writes=["maskS"])
        S.dma("sp", lambda e: e.dma_start(out=ind[:], in_=c_ind[:, :]), writes=["ind"])
        S.dma("sp", lambda e: e.dma_start(out=gpre[:], in_=gpre_fm[:, :, :]), writes=["gpre"])
        S.dma("sp", lambda e: e.dma_start(out=pscale[:], in_=pscale_fm[:, :]), writes=["pscale"])

        xt = R2[:, 0:2048]
        xs = R2[:, 2048:4096]
        grow = R2[:, 4096:6144]
        tmpb = R2[:, 6144:8192]
        junk = R2[:, 8192:9216].bitcast(BF16)
        WB1f = WB[1][:].bitcast(F32)
        epsc = small[:, 40:41]
        S.op("dve", lambda e: e.memset(epsc, EPS), writes=["epsc"])
        par = [0]
        mode = ["A"]

        def tset():
            p = par[0] % 2
            par[0] += 1
            if p == 0:
                a, b_ = R2[:, 0:2048], R2[:, 2048:4096]
            elif mode[0] == "A":
                a, b_ = R2[:, 4096:6144], R2[:, 6144:8192]
            else:
                a, b_ = WB1f[:, 0:2048], WB1f[:, 2048:4096]
            o8 = 8 * p
            return a, b_, small[:, o8:o8 + 1], small[:, o8 + 1:o8 + 2], small[:, o8 + 2:o8 + 3], str(p)

        def rstd_of(src, n, srckeys, T):
            _, _, ss, lv, rstd, sx = T
            S.op("act", lambda e: e.activation(out=junk[:n], in_=src, func=AF.Square, accum_out=ss[:n]),
                 reads=srckeys, writes=["ss" + sx])
            S.op("act", lambda e: e.activation(out=lv[:n], in_=ss[:n], func=AF.Ln, bias=epsc[:n], scale=1.0 / D),
                 reads=["ss" + sx, "epsc"], writes=["lv" + sx])
            S.op("act", lambda e: e.activation(out=rstd[:n], in_=lv[:n], func=AF.Exp, scale=-0.5),
                 reads=["lv" + sx], writes=["rstd" + sx])

        def to_fm_a(src, n, srckeys, T):
            _, xs_, _, _, rstd, sx = T
            S.op("dve", lambda e: e.tensor_scalar(out=xs_[:n], in0=src, scalar1=rstd[:n], scalar2=None, op0=ALU.mult),
                 reads=list(srckeys) + ["rstd" + sx], writes=["xs" + sx])

        def to_fm_b(n, dstT, col0, gidx, dstkey, T):
            _, xs_, _, _, rstd, sx = T
            for g4 in range(4):
                b = nbank()
                pt = banks[b][:].rearrange("p (j n) -> p j n", j=4)
                for j in range(4):
                    c = g4 * 4 + j
                    S.op("pe", (lambda c=c, j=j, pt=pt: lambda e: e.transpose(out=pt[:, j, :n], in_=xs_[:n, c * 128:(c + 1) * 128], identity=ident[:n, :n]))(),
                         reads=["xs" + sx, "ident"], writes=["pb%d" % b])
                for j in range(4):
                    c = g4 * 4 + j
                    fk = "%s_%d_%d" % (dstkey, c, col0)
                    if g4 == 0 or (mode[0] == "B" and g4 == 1):
                        S.op("act", (lambda c=c, j=j, pt=pt: lambda e: e.activation(out=dstT[:, c, col0:col0 + n], in_=pt[:, j, :n], func=AF.Copy, scale=gpre[:, gidx, c:c + 1]))(),
                             reads=["pb%d" % b, "gpre"], writes=[fk])
                    else:
                        S.op("dve", (lambda c=c, j=j, pt=pt: lambda e: e.tensor_scalar(out=dstT[:, c, col0:col0 + n], in0=pt[:, j, :n], scalar1=gpre[:, gidx, c:c + 1], scalar2=None, op0=ALU.mult))(),
                             reads=["pb%d" % b, "gpre"], writes=[fk])

        def pipeline(stages):
            prev = None
            for (sa, sb_) in stages:
                sa()
                if prev is not None:
                    prev()
                prev = sb_
            if prev is not None:
                prev()

        def prenorm_stage(src_rows, n, dstT, col0, gidx, dstkey):
            T = tset()
            xt_, sx = T[0], T[5]

            def sa():
                S.dma("sp", lambda e: e.dma_start(out=xt_[:n], in_=src_rows), writes=["xt" + sx])
                rstd_of(xt_[:n], n, ["xt" + sx], T)
                to_fm_a(xt_[:n], n, ["xt" + sx], T)

            def sb_():
                to_fm_b(n, dstT, col0, gidx, dstkey, T)
            return sa, sb_

        def pipeline3(stages):
            n_ = len(stages)
            for i in range(n_ + 2):
                if i < n_:
                    stages[i][0]()
                if 0 <= i - 1 < n_:
                    stages[i - 1][1]()
                if 0 <= i - 2 < n_:
                    stages[i - 2][2]()

        def boundary_stage(t, n, hsrc_rows, hdst_rows, ydst_rows, post_idx, next_gidx, dstT, col0):
            T = tset()
            sx = T[5]
            o8 = 8 * int(sx)
            T2 = (T[0], T[1], small[:, o8 + 3:o8 + 4], small[:, o8 + 4:o8 + 5], small[:, o8 + 5:o8 + 6], sx + "b")
            rstd = T[4]
            a = acc[:n, t, :]

            xt_ = T[0]

            def s1():
                S.dma("sp", lambda e: e.dma_start(out=xt_[:n], in_=hsrc_rows), writes=["xt" + sx])
                if post_idx in (0, 2):
                    ss_, lv_ = T[2], T[3]
                    S.op("dve", lambda e: e.tensor_reduce(out=ss_[:n], in_=ssp[:n, 4 * t:4 * t + 4], axis=mybir.AxisListType.X, op=ALU.add),
                         reads=["ssp%d" % t], writes=["ss" + sx])
                    S.op("act", lambda e: e.activation(out=lv_[:n], in_=ss_[:n], func=AF.Ln, bias=epsc[:n], scale=1.0 / D),
                         reads=["ss" + sx, "epsc"], writes=["lv" + sx])
                    S.op("act", lambda e: e.activation(out=rstd[:n], in_=lv_[:n], func=AF.Exp, scale=-0.5),
                         reads=["lv" + sx], writes=["rstd" + sx])
                    S.op("dve", lambda e: e.scalar_tensor_tensor(out=xt_[:n], in0=a, scalar=rstd[:n], in1=xt_[:n], op0=ALU.mult, op1=ALU.add),
                         reads=["acc%d" % t, "rstd" + sx, "xt" + sx], writes=["xt" + sx])
                    return
                rstd_of(a, n, ["acc%d" % t], T)
                S.op("dve", lambda e: e.scalar_tensor_tensor(out=a, in0=a, scalar=rstd[:n], in1=grow[:n], op0=ALU.mult, op1=ALU.mult),
                     reads=["acc%d" % t, "rstd" + sx, "grow"], writes=["acc%d" % t])
                S.op("pool", lambda e: e.tensor_tensor(out=xt_[:n], in0=a, in1=xt_[:n], op=ALU.add),
                     reads=["acc%d" % t, "xt" + sx], writes=["xt" + sx])

            def s2():
                if hdst_rows is not None:
                    S.dma("sp", lambda e: e.dma_start(out=hdst_rows, in_=xt_[:n]), reads=["xt" + sx], writes=["hscr%d" % t])
                if ydst_rows is not None:
                    S.dma("sp", lambda e: e.dma_start(out=ydst_rows, in_=xt_[:n]), reads=["xt" + sx], writes=["yout"])
                if next_gidx is not None:
                    rstd_of(xt_[:n], n, ["xt" + sx], T2)
                    xs_ = T2[1]
                    S.op("dve", lambda e: e.tensor_scalar(out=xs_[:n], in0=xt_[:n], scalar1=T2[4][:n], scalar2=None, op0=ALU.mult),
                         reads=["xt" + sx, "rstd" + sx + "b"], writes=["xs" + sx])

            def s3():
                if next_gidx is not None:
                    to_fm_b(n, dstT, col0, next_gidx, "xnT", T)
            return s1, s2, s3

        def load_grow(idx):
            S.dma("sp", lambda e: e.dma_start(out=grow, in_=gpost[idx:idx + 1, :].partition_broadcast(128)[:, 0, :]),
                  writes=["grow"])

        growM = MISC[:, 0:2048]
        junkM = MISC[:, 2048:2304].bitcast(BF16)
        ssp = sb("ssp", [128, 40], F32)

        def proj_to_acc(actT, actkey, wsrc, tiles, tag=None):
            for cg in range(4):
                wv, wk = load_w(wsrc[:, cg * 512:(cg + 1) * 512], 16, 512, tag=(tag if cg == 0 else None))
                for t in tiles:
                    col0, n = TILES[t]
                    b = nbank()
                    for kc in range(16):
                        S.op("pe", (lambda kc=kc, b=b, col0=col0, n=n, wv=wv: lambda e: e.matmul(banks[b][:n, :], lhsT=actT[:, kc, col0:col0 + n], rhs=wv[:, kc, :], start=(kc == 0), stop=(kc == 15)))(),
                             reads=[actkey, wk], writes=["pb%d" % b])
                    S.op("act", (lambda b=b, t=t, n=n, cg=cg: lambda e: e.activation(out=junkM[:n], in_=banks[b][:n, :], func=AF.Square, accum_out=ssp[:n, 4 * t + cg:4 * t + cg + 1]))(),
                         reads=["pb%d" % b], writes=["ssp%d" % t, "ser%d" % b])
                    S.op("dve", (lambda b=b, t=t, n=n, cg=cg: lambda e: e.tensor_tensor(out=acc[:n, t, cg * 512:(cg + 1) * 512], in0=banks[b][:n, :], in1=growM[:n, cg * 512:(cg + 1) * 512], op=ALU.mult))(),
                         reads=["pb%d" % b, "ser%d" % b, "growM"], writes=["acc%d" % t])

        def ffn(layer, tiles):
            hid = R2[:, 0:2336].bitcast(BF16).rearrange("p (c n) -> p c n", c=4)
            wd_all = R2[:, 2336:2336 + 6144].bitcast(BF16).rearrange("p (s k c) -> p s k c", s=6, k=4)
            wd_ctr = [0]
            for g in range(16):
                wu, wuk = load_w(w_up[layer, :, g * 512:(g + 1) * 512], 16, 512, tag=("up%d" % layer if g == 0 else None))
                for fc in range(4):
                    for (c0, ncol) in TGS:
                        b = nbank()
                        for kc in range(16):
                            S.op("pe", (lambda kc=kc, b=b, c0=c0, ncol=ncol, fc=fc, wu=wu: lambda e: e.matmul(banks[b][:, :ncol], lhsT=wu[:, kc, fc * 128:(fc + 1) * 128], rhs=XNT[:, kc, c0:c0 + ncol], start=(kc == 0), stop=(kc == 15)))(),
                                 reads=["xnT", wuk], writes=["pb%d" % b])
                        S.op("act", (lambda b=b, c0=c0, ncol=ncol, fc=fc: lambda e: e.activation(out=hid[:, fc, c0:c0 + ncol], in_=banks[b][:, :ncol], func=AF.Relu))(),
                             reads=["pb%d" % b], writes=["hid%d" % fc])
                        S.op("act", (lambda c0=c0, ncol=ncol, fc=fc: lambda e: e.activation(out=hid[:, fc, c0:c0 + ncol], in_=hid[:, fc, c0:c0 + ncol], func=AF.Square))(),
                             reads=["hid%d" % fc], writes=["hid%d" % fc])
                for cg in range(4):
                    s = wd_ctr[0] % 6
                    wd_ctr[0] += 1
                    wdv = wd_all[:, s]
                    S.dma("pool", (lambda wdv=wdv, g=g, cg=cg: lambda e: e.dma_start(out=wdv, in_=w_down[layer, g * 512:(g + 1) * 512, cg * 512:(cg + 1) * 512].rearrange("(k p) c -> p k c", p=128)))(),
                          writes=["wd%d" % s])
                    for t in tiles:
                        col0, n = TILES[t]
                        b = nbank()
                        for kc in range(4):
                            S.op("pe", (lambda kc=kc, b=b, col0=col0, n=n, wdv=wdv: lambda e: e.matmul(banks[b][:n, :], lhsT=hid[:, kc, col0:col0 + n], rhs=wdv[:, kc, :], start=(kc == 0), stop=(kc == 3)))(),
                                 reads=["hid%d" % kc, "wd%d" % s], writes=["pb%d" % b])
                        a = acc[:n, t, cg * 512:(cg + 1) * 512]
                        if g == 0:
                            S.op("dve", (lambda b=b, a=a, n=n: lambda e: e.tensor_copy(out=a, in_=banks[b][:n, :]))(),
                                 reads=["pb%d" % b], writes=["acc%d_%d" % (t, cg)])
                        else:
                            S.op("dve", (lambda b=b, a=a, n=n: lambda e: e.tensor_tensor(out=a, in0=a, in1=banks[b][:n, :], op=ALU.add))(),
                                 reads=["pb%d" % b, "acc%d_%d" % (t, cg)], writes=["acc%d_%d" % (t, cg)])


        def mm(out, lhsT, rhs, start, stop, reads, writes):
            S.op("pe", lambda e: e.matmul(out, lhsT=lhsT, rhs=rhs, start=start, stop=stop), reads=reads, writes=writes)

        XPRE = ACC[:, 0:8192].bitcast(BF16).rearrange("p (c n) -> p c n", c=16)
        pipeline([prenorm_stage(xin[128 * t:128 * t + TILES[t][1], :], TILES[t][1], XNT, TILES[t][0], 0, "xnT") for t in range(10)]
                 + [prenorm_stage(xpre[128 * t:128 * (t + 1), :], 128, XPRE, 128 * t, 0, "xpre") for t in range(8)])
        S.barrier()

        NG = 2192
        LI = ACC[0:4, 8192:8192 + NG]
        LF = ACC[0:4, 10384:10384 + NG]
        BB = ACC[0:4, 12576:12576 + NG]
        GG = ACC[0:4, 14768:14768 + NG]
        EE = ACC[0:4, 16960:16960 + NG]
        ZER = MISC[0:4, 0:NG]
        gsm = sb("gsm", [4, 256], F32)
        gbi, gbf, nbf = gsm[:, 0:1], gsm[:, 1:2], gsm[:, 2:3]
        smt = gsm[:, 4:20]
        NGE = gsm[:, 20:52]
        CW = gsm[:, 52:61]
        CS = gsm[:, 64:80]
        MP = gsm[:, 80:81]
        MS = gsm[:, 84:100]
        XD = gsm[:, 100:136]
        XDS = gsm[:, 136:200]
        ones4 = sb("ones4", [4, 128], F32)
        dectm = sb("dectm", [128, 18, 4], F32)
        floortm = sb("floortm", [128, 10, 4], F32)
        cwb = sb("cwb", [128, 36], F32)
        csb = sb("csb", [128, 64], F32)
        S.dma("sp", lambda e: e.dma_start(out=gbi, in_=gb_i[:, :]), writes=["gsm"])
        S.dma("sp", lambda e: e.dma_start(out=gbf, in_=gb_f[:, :]), writes=["gsm"])
        S.dma("sp", lambda e: e.dma_start(out=smt, in_=smT[:, :]), writes=["gsm"])
        S.op("dve", lambda e: e.tensor_scalar(out=nbf, in0=gbf, scalar1=-1.0, scalar2=None, op0=ALU.mult), reads=["gsm"], writes=["gsm"])
        S.op("dve", lambda e: e.memset(ZER, 0.0), writes=["zer"])
        S.op("dve", lambda e: e.memset(ones4[:], 1.0), writes=["ones4"])
        wg, wgk = load_w(w_in0[:, 6144:6152], 16, 8)

        def gate_proj(actT, actkey, c0, ncol, gcol):
            b1, b2 = nbank(), nbank()
            for kc in range(16):
                mm(banks[b1][0:4, :ncol], wg[:, kc, 0:4], actT[:, kc, c0:c0 + ncol], kc == 0, kc == 15, [actkey, wgk], ["pb%d" % b1])
            for kc in range(16):
                mm(banks[b2][0:4, :ncol], wg[:, kc, 4:8], actT[:, kc, c0:c0 + ncol], kc == 0, kc == 15, [actkey, wgk], ["pb%d" % b2])
            S.op("act", lambda e: e.activation(out=LI[:, gcol:gcol + ncol], in_=banks[b1][0:4, :ncol], func=AF.Identity, bias=gbi),
                 reads=["pb%d" % b1, "gsm"], writes=["LI"])
            S.op("act", lambda e: e.activation(out=LF[:, gcol:gcol + ncol], in_=banks[b2][0:4, :ncol], func=AF.Exp, bias=nbf, scale=-1.0),
                 reads=["pb%d" % b2, "gsm"], writes=["LF"])

        for (c0, ncol) in TGS:
            gate_proj(XNT, "xnT", c0, ncol, 1024 + c0)
        for c0 in (0, 512):
            gate_proj(XPRE, "xpre", c0, 512, c0)
        S.op("dve", lambda e: e.tensor_scalar(out=LF, in0=LF, scalar1=1.0, scalar2=None, op0=ALU.add), reads=["LF"], writes=["LF"])
        S.op("act", lambda e: e.activation(out=LF, in_=LF, func=AF.Ln), reads=["LF"], writes=["LF"])
        S.op("dve", lambda e: e.tensor_tensor_scan(out=BB[:, 0:2064], data0=LF[:, 0:2064], data1=ZER[:, 0:2064], initial=0.0, op0=ALU.add, op1=ALU.add),
             reads=["LF", "zer"], writes=["BB"])
        for q in range(16):
            a0 = 2064 + 8 * q
            S.op("dve", (lambda a0=a0: lambda e: e.tensor_tensor_scan(out=BB[:, a0:a0 + 8], data0=LF[:, a0:a0 + 8], data1=ZER[:, 0:8], initial=0.0, op0=ALU.add, op1=ALU.add))(),
                 reads=["LF", "zer"], writes=["BB"])
        S.op("dve", lambda e: e.tensor_tensor(out=LI, in0=LI, in1=BB, op=ALU.add), reads=["LI", "BB"], writes=["LI"])
        S.op("dve", lambda e: e.tensor_tensor_scan(out=GG[:, 0:2064], data0=LI[:, 0:2064], data1=LI[:, 0:2064], initial=0.0, op0=ALU.max, op1=ALU.max),
             reads=["LI"], writes=["GG"])
        for q in range(16):
            a0 = 2064 + 8 * q
            S.op("dve", (lambda a0=a0, q=q: lambda e: e.tensor_tensor_scan(out=GG[:, a0:a0 + 8], data0=LI[:, a0:a0 + 8], data1=LI[:, a0:a0 + 8], initial=smt[:, q:q + 1], op0=ALU.max, op1=ALU.max))(),
                 reads=["LI", "gsm"], writes=["GG"])
        CH = [(0, 1024), (1024, 16)] + [(1040 + 128 * t, 128) for t in range(8)] + [(2064 + 8 * q, 8) for q in range(16)]
        S.op("dve", lambda e: e.tensor_scalar(out=NGE[:, 0:1], in0=GG[:, 1023:1024], scalar1=-1.0, scalar2=None, op0=ALU.mult), reads=["GG"], writes=["NGE"])
        S.op("dve", lambda e: e.tensor_scalar(out=NGE[:, 1:2], in0=GG[:, 1039:1040], scalar1=-1.0, scalar2=None, op0=ALU.mult), reads=["GG"], writes=["NGE"])
        S.op("dve", lambda e: e.tensor_scalar(out=NGE[:, 2:10], in0=GG[:, 1040:2064].rearrange("p (j n) -> p j n", n=128)[:, :, 127], scalar1=-1.0, scalar2=None, op0=ALU.mult), reads=["GG"], writes=["NGE"])
        S.op("dve", lambda e: e.tensor_scalar(out=NGE[:, 10:26], in0=GG[:, 2064:2192].rearrange("p (q r) -> p q r", r=8)[:, :, 7], scalar1=-1.0, scalar2=None, op0=ALU.mult), reads=["GG"], writes=["NGE"])
        def chunk_bias_add(dst, src, key_r, key_w, extra=()):
            S.op("dve", lambda e: e.tensor_scalar(out=dst[:, 0:1024], in0=src[:, 0:1024], scalar1=NGE[:, 0:1], scalar2=None, op0=ALU.add), reads=[key_r, "NGE"] + list(extra), writes=[key_w])
            S.op("dve", lambda e: e.tensor_scalar(out=dst[:, 1024:1040], in0=src[:, 1024:1040], scalar1=NGE[:, 1:2], scalar2=None, op0=ALU.add), reads=[key_r, "NGE"] + list(extra), writes=[key_w])
            S.op("dve", lambda e: e.tensor_tensor(out=dst[:, 1040:2064].rearrange("p (j n) -> p j n", n=128), in0=src[:, 1040:2064].rearrange("p (j n) -> p j n", n=128),
                                                  in1=NGE[:, 2:10].unsqueeze(2).to_broadcast([4, 8, 128]), op=ALU.add), reads=[key_r, "NGE"] + list(extra), writes=[key_w])
            S.op("dve", lambda e: e.tensor_tensor(out=dst[:, 2064:2192].rearrange("p (q r) -> p q r", r=8), in0=src[:, 2064:2192].rearrange("p (q r) -> p q r", r=8),
                                                  in1=NGE[:, 10:26].unsqueeze(2).to_broadcast([4, 16, 8]), op=ALU.add), reads=[key_r, "NGE"] + list(extra), writes=[key_w])
        chunk_bias_add(EE, LI, "LI", "EE")
        S.op("act", lambda e: e.activation(out=EE, in_=EE, func=AF.Exp), reads=["EE"], writes=["EE"])
        chunk_bias_add(LF, BB, "BB", "FL", extra=["LF"])
        S.op("act", lambda e: e.activation(out=LF, in_=LF, func=AF.Exp), reads=["FL"], writes=["FL"])
        S.op("dve", lambda e: e.tensor_tensor(out=CW, in0=NGE[:, 1:10], in1=NGE[:, 0:9], op=ALU.subtract), reads=["NGE"], writes=["CW"])
        S.op("act", lambda e: e.activation(out=CW, in_=CW, func=AF.Exp), reads=["CW"], writes=["CW"])
        S.op("dve", lambda e: e.tensor_tensor(out=CS, in0=NGE[:, 10:26], in1=smt, op=ALU.add), reads=["NGE", "gsm"], writes=["CS"])
        S.op("act", lambda e: e.activation(out=CS, in_=CS, func=AF.Exp), reads=["CS"], writes=["CS"])
        S.op("dve", lambda e: e.scalar_tensor_tensor(out=MP, in0=NGE[:, 9:10], scalar=-1.0, in1=BB[:, 2063:2064], op0=ALU.mult, op1=ALU.subtract),
             reads=["NGE", "BB"], writes=["MP"])
        S.dma("sp", lambda e: e.dma_start(out=o_mp[:, :], in_=MP), reads=["MP"], writes=["yout"])
        S.op("dve", lambda e: e.scalar_tensor_tensor(out=MS, in0=NGE[:, 10:26], scalar=-1.0, in1=BB[:, 2064:2192].rearrange("p (q r) -> p q r", r=8)[:, :, 7], op0=ALU.mult, op1=ALU.subtract),
             reads=["NGE", "BB"], writes=["MS"])
        S.dma("sp", lambda e: e.dma_start(out=o_msT[:, :], in_=MS), reads=["MS"], writes=["yout"])
        S.op("dve", lambda e: e.tensor_tensor(out=XD.rearrange("p (j h) -> p j h", h=4), in0=CW.unsqueeze(2).to_broadcast([4, 9, 4]), in1=ident[0:4, 0:4].unsqueeze(1).to_broadcast([4, 9, 4]), op=ALU.mult),
             reads=["CW", "ident"], writes=["XD"])
        S.op("dve", lambda e: e.tensor_tensor(out=XDS.rearrange("p (j h) -> p j h", h=4), in0=CS.unsqueeze(2).to_broadcast([4, 16, 4]), in1=ident[0:4, 0:4].unsqueeze(1).to_broadcast([4, 16, 4]), op=ALU.mult),
             reads=["CS", "ident"], writes=["XDS"])
        b = nbank()
        mm(banks[b][:, 0:36], ones4[:, :], XD, True, True, ["ones4", "XD"], ["pb%d" % b])
        mm(banks[b][:, 64:128], ones4[:, :], XDS, True, True, ["ones4", "XDS"], ["pb%d" % b])
        S.op("dve", (lambda b=b: lambda e: e.tensor_copy(out=cwb[:], in_=banks[b][:, 0:36]))(), reads=["pb%d" % b], writes=["cwb"])
        S.op("dve", (lambda b=b: lambda e: e.tensor_copy(out=csb[:], in_=banks[b][:, 64:128]))(), reads=["pb%d" % b], writes=["csb"])
        GT = [(1024 + TILES[t][0], TILES[t][1]) for t in range(10)] + [(128 * t, 128) for t in range(8)]
        for ti, (a0, nn) in enumerate(GT):
            b = nbank()
            S.op("pe", (lambda b=b, a0=a0, nn=nn: lambda e: e.transpose(out=banks[b][:nn, 0:4], in_=EE[:, a0:a0 + nn], identity=ident[0:4, 0:4]))(),
                 reads=["EE", "ident"], writes=["pb%d" % b])
            S.op("dve", (lambda b=b, ti=ti, nn=nn: lambda e: e.tensor_copy(out=dectm[:nn, ti, :], in_=banks[b][:nn, 0:4]))(), reads=["pb%d" % b], writes=["dectm"])
            if ti < 10:
                b = nbank()
                S.op("pe", (lambda b=b, a0=a0, nn=nn: lambda e: e.transpose(out=banks[b][:nn, 0:4], in_=LF[:, a0:a0 + nn], identity=ident[0:4, 0:4]))(),
                     reads=["FL", "ident"], writes=["pb%d" % b])
                S.op("dve", (lambda b=b, ti=ti, nn=nn: lambda e: e.tensor_copy(out=floortm[:nn, ti, :], in_=banks[b][:nn, 0:4]))(), reads=["pb%d" % b], writes=["floortm"])
        prefetch_w("pk0", w_in0[:, 1024:1280], 16, 256)
        S.barrier()

        CST = ACC[:, 16368:20480].rearrange("p (h c e) -> p h c e", h=4, c=2)
        S.op("dve", lambda e: e.memset(ACC[:, 16368:20480], 0.0), writes=["cst0", "cst1", "cst2", "cst3"])
        kpre = ACC[:, 8192:9216].bitcast(BF16).rearrange("p (t d) -> p t d", t=8)
        vdpre = ACC[:, 9216:11272].bitcast(BF16).rearrange("p (t d) -> p t d", t=8)
        for h in range(4):
            wk, wkk = load_w(w_in0[:, 1024 + 256 * h:1280 + 256 * h], 16, 256, tag=("pk0" if h == 0 else None))
            for t in range(8):
                b = nbank()
                for kc in range(16):
                    mm(banks[b][:, 0:256], XPRE[:, kc, 128 * t:128 * (t + 1)], wk[:, kc, :], kc == 0, kc == 15, ["xpre", wkk], ["pb%d" % b])
                S.op("act", (lambda b=b, t=t: lambda e: e.activation(out=kpre[:, t, :], in_=banks[b][:, 0:256], func=AF.Copy, scale=0.0625))(),
                     reads=["pb%d" % b], writes=["kpre"])
            wv, wvk = load_w(w_in0[:, 2048 + 512 * h:2560 + 512 * h], 16, 512)
            for t in range(8):
                b = nbank()
                for kc in range(16):
                    mm(banks[b][:, :], XPRE[:, kc, 128 * t:128 * (t + 1)], wv[:, kc, :], kc == 0, kc == 15, ["xpre", wvk], ["pb%d" % b])
                S.op("act", (lambda b=b, t=t, h=h: lambda e: e.activation(out=vdpre[:, t, 0:512], in_=banks[b][:, :], func=AF.Copy, scale=dectm[:, 10 + t, h:h + 1]))(),
                     reads=["pb%d" % b, "dectm"], writes=["vdpre"])
                S.op("dve", (lambda t=t, h=h: lambda e: e.tensor_copy(out=vdpre[:, t, 512:513], in_=dectm[:, 10 + t, h:h + 1]))(),
                     reads=["dectm"], writes=["vdpre"])
            for dc in range(2):
                b, b2 = nbank(), nbank()
                for t in range(8):
                    mm(banks[b][:, :], kpre[:, t, dc * 128:(dc + 1) * 128], vdpre[:, t, 0:512], t == 0, t == 7, ["kpre", "vdpre"], ["pb%d" % b])
                for t in range(8):
                    mm(banks[b2][:, 0:1], kpre[:, t, dc * 128:(dc + 1) * 128], vdpre[:, t, 512:513], t == 0, t == 7, ["kpre", "vdpre"], ["pb%d" % b2])
                S.op("dve", (lambda b=b, h=h, dc=dc: lambda e: e.tensor_copy(out=CST[:, h, dc, 0:512], in_=banks[b][:, :]))(), reads=["pb%d" % b], writes=["cst%d" % h])
                S.op("dve", (lambda b2=b2, h=h, dc=dc: lambda e: e.tensor_copy(out=CST[:, h, dc, 512:513], in_=banks[b2][:, 0:1]))(), reads=["pb%d" % b2], writes=["cst%d" % h])
        prefetch_w("q0", w_in0[:, 0:256], 16, 256)
        S.barrier()

        o = 0
        def carve(ncols, dt=F32):
            nonlocal o
            v = ACC[:, o:o + ncols]
            o += ncols
            return v if dt == F32 else v.bitcast(dt)
        qT = carve(1168, BF16).rearrange("p (c n) -> p c n", c=2)
        kT = carve(1168, BF16).rearrange("p (c n) -> p c n", c=2)
        ktm = carve(1280, BF16).rearrange("p (t d) -> p t d", t=10)
        vdtm = carve(2570, BF16).rearrange("p (t d) -> p t d", t=10)
        og = carve(2560, BF16).rearrange("p (t d) -> p t d", t=10)
        CbL = [carve(514, BF16).rearrange("p (c e) -> p c e", c=2) for _ in range(2)]
        STbL = [carve(64, BF16) for _ in range(2)]
        hrowL = [carve(512) for _ in range(2)]
        NCQ = 2
        CqL = [carve(1024).rearrange("p (c e) -> p c e", c=2) for _ in range(NCQ)]
        CoL = [carve(1024).rearrange("p (c e) -> p c e", c=2) for _ in range(2)]
        CqbL = [carve(512, BF16).rearrange("p (c e) -> p c e", c=2) for _ in range(1)]
        k8s = carve(128, BF16)
        v8s = carve(257, BF16)
        kzL = [carve(128, BF16) for _ in range(2)]
        assert o <= 16368, o
        etmp = MISC[:, 2048:2304].bitcast(BF16)
        og8s = MISC[:, 2304:2560].bitcast(BF16)
        nraw = MISC[:, 2560:2688].rearrange("p (c n) -> p c n", c=2)
        nnew = MISC[:, 2688:2816].rearrange("p (c n) -> p c n", c=2)
        nsb = MISC[:, 2816:2880].bitcast(BF16).rearrange("p (c n) -> p c n", c=2)
        nl = hrowL[0][0:64, 0:256]
        qz = MISC[:, 0:2048].bitcast(BF16).rearrange("p (c q n) -> p c q n", c=2, q=16)
        qzf = MISC[:, 0:2048].bitcast(BF16).rearrange("p (c n) -> p c n", c=2)
        HGT = R2[:].bitcast(BF16).rearrange("p (c n) -> p c n", c=16)
        hsm = sb("hsm", [128, 32], F32)
        eps2 = hsm[:, 30:31]
        S.op("dve", lambda e: e.memset(eps2, EPS), writes=["eps2"])
        S.op("dve", lambda e: e.memset(qz, 0.0), writes=["qz"])
        hgfm = sb("hgfm", [128, 16], F32)
        S.dma("sp", lambda e: e.dma_start(out=hgfm[:], in_=headg[:, :]), writes=["hgfm"])
        S.dma("sp", lambda e: e.dma_start(out=nl, in_=sn.rearrange("q h d -> (q h) d")), writes=["hrow0"])
        bn_ = nbank()
        for dc in range(2):
            S.op("pe", (lambda dc=dc: lambda e: e.transpose(out=banks[bn_][:, dc * 64:(dc + 1) * 64], in_=nl[:, dc * 128:(dc + 1) * 128], identity=ident[0:64, 0:64]))(),
                 reads=["hrow0", "ident"], writes=["pb%d" % bn_])
        S.op("dve", lambda e: e.tensor_tensor(out=nraw, in0=banks[bn_][:, 0:128].rearrange("p (c n) -> p c n", c=2), in1=csb[:, :].unsqueeze(1).to_broadcast([128, 2, 64]), op=ALU.mult),
             reads=["pb%d" % bn_, "csb"], writes=["nraw"])
        S.op("act", lambda e: e.activation(out=nsb, in_=nraw, func=AF.Copy), reads=["nraw"], writes=["nsb"])
        BS, BA, BD, BC0, BC1, BN = 0, (1, 2), (3, 4), 5, 6, 7
        hctr = [0]

        def head_out_a(h, t, L, bA, bD, floor_ap, gate=None, gkey=None):
            p = hctr[0] % 2
            hctr[0] += 1
            o8 = 8 * p
            dn, rec, ssh, lvh, rsh, rr = (hsm[:, o8 + i:o8 + i + 1] for i in range(6))
            sx = "h%d" % p
            S.op("act", lambda e: e.activation(out=dn[:L], in_=banks[bD][:L, 0:1], func=AF.Abs), reads=["pb%d" % bD], writes=["dn" + sx])
            S.op("dve", lambda e: e.tensor_scalar(out=dn[:L], in0=dn[:L], scalar1=floor_ap, scalar2=None, op0=ALU.max),
                 reads=["dn" + sx, "floortm"], writes=["dn" + sx])
            S.op("dve", lambda e: e.reciprocal(out=rec[:L], in_=dn[:L]), reads=["dn" + sx], writes=["rec" + sx])
            S.op("act", lambda e: e.activation(out=etmp[:L], in_=banks[bA][:L, :], func=AF.Square, scale=rec[:L], accum_out=ssh[:L]),
                 reads=["pb%d" % bA, "rec" + sx], writes=["ssh" + sx, "etmp"])
            S.op("act", lambda e: e.activation(out=lvh[:L], in_=ssh[:L], func=AF.Ln, bias=eps2[:L], scale=1.0 / 512), reads=["ssh" + sx, "eps2"], writes=["lvh" + sx])
            S.op("act", lambda e: e.activation(out=rsh[:L], in_=lvh[:L], func=AF.Exp, scale=-0.5), reads=["lvh" + sx], writes=["rsh" + sx])
            return (t, L, bA, rec, rsh, rr, sx, gate, gkey)

        def head_out_b(st_):
            t, L, bA, rec, rsh, rr, sx, gate, gkey = st_
            g_ = og[:L, t, :] if gate is None else gate[:L]
            gk_ = ("og%d" % t) if gkey is None else gkey
            S.op("dve", lambda e: e.tensor_tensor(out=rr[:L], in0=rec[:L], in1=rsh[:L], op=ALU.mult), reads=["rec" + sx, "rsh" + sx], writes=["rr" + sx])
            S.op("dve", lambda e: e.scalar_tensor_tensor(out=g_, in0=banks[bA][:L, :], scalar=rr[:L], in1=g_, op0=ALU.mult, op1=ALU.mult),
                 reads=["pb%d" % bA, "rr" + sx, gk_], writes=[gk_])

        def head_out(*args, **kw):
            head_out_b(head_out_a(*args, **kw))

        def proj_steps(h):
            steps = []
            holder = {}

            def ld(key, col, ncol):
                def f():
                    holder[key] = load_w(w_in0[:, col:col + ncol], 16, ncol, tag=("q0" if (h == 0 and key == "qT") else None))
                return f
            for (wcol, dstT, scl, key) in ((256 * h, qT, 1.0, "qT"), (1024 + 256 * h, kT, 0.0625, "kT")):
                steps.append(ld(key, wcol, 256))
                for dc in range(2):
                    for (c0, ncol) in TGS:
                        def f(dc=dc, c0=c0, ncol=ncol, dstT=dstT, scl=scl, key=key):
                            wq, wqk = holder[key]
                            b = nbank()
                            for kc in range(16):
                                mm(banks[b][:, :ncol], wq[:, kc, dc * 128:(dc + 1) * 128], XNT[:, kc, c0:c0 + ncol], kc == 0, kc == 15, ["xnT", wqk], ["pb%d" % b])
                            S.op("act", lambda e: e.activation(out=dstT[:, dc, c0:c0 + ncol], in_=banks[b][:, :ncol], func=AF.Copy, scale=scl),
                                 reads=["pb%d" % b], writes=[key])
                        steps.append(f)
                if key == "kT":
                    for t in range(10):
                        def f(t=t):
                            wq, wqk = holder["kT"]
                            col0, n = TILES[t]
                            b = nbank()
                            for kc in range(16):
                                mm(banks[b][:n, 0:256], XNT[:, kc, col0:col0 + n], wq[:, kc, :], kc == 0, kc == 15, ["xnT", wqk], ["pb%d" % b])
                            S.op("act", lambda e: e.activation(out=ktm[:n, t, :], in_=banks[b][:n, 0:256], func=AF.Copy, scale=0.0625),
                                 reads=["pb%d" % b], writes=["ktm%d" % t])
                        steps.append(f)
            steps.append(ld("v", 2048 + 512 * h, 512))
            for t in range(10):
                def f(t=t):
                    wv, wvk = holder["v"]
                    col0, n = TILES[t]
                    b = nbank()
                    for kc in range(16):
                        mm(banks[b][:n, :], XNT[:, kc, col0:col0 + n], wv[:, kc, :], kc == 0, kc == 15, ["xnT", wvk], ["pb%d" % b])
                    S.op("act", lambda e: e.activation(out=vdtm[:n, t, 0:512], in_=banks[b][:n, :], func=AF.Copy, scale=dectm[:n, t, h:h + 1]),
                         reads=["pb%d" % b, "dectm"], writes=["vdtm%d" % t])
                    S.op("dve", lambda e: e.tensor_copy(out=vdtm[:n, t, 512:513], in_=dectm[:n, t, h:h + 1]), reads=["dectm"], writes=["vdtm%d" % t])
                steps.append(f)
            steps.append(ld("o", 4096 + 512 * h, 512))
            for t in range(10):
                def f(t=t):
                    wo, wok = holder["o"]
                    col0, n = TILES[t]
                    b = nbank()
                    for kc in range(16):
                        mm(banks[b][:n, :], XNT[:, kc, col0:col0 + n], wo[:, kc, :], kc == 0, kc == 15, ["xnT", wok], ["pb%d" % b])
                    S.op("act", lambda e: e.activation(out=og[:n, t, :], in_=banks[b][:n, :], func=AF.Sigmoid), reads=["pb%d" % b], writes=["og%d" % t])
                steps.append(f)
            return steps

        def hgt_tiles(h, tiles, src8=None):
            for ti in tiles:
                col0_, n = TILES[ti]
                b = nbank()
                ptb = banks[b][:, 0:256].bitcast(BF16).rearrange("p (j n) -> p j n", j=4)
                for c in range(4):
                    src_ = og[:n, ti, c * 128:(c + 1) * 128] if src8 is None else src8[:n, c * 128:(c + 1) * 128]
                    S.op("pe", (lambda c=c, n=n, ptb=ptb, src_=src_: lambda e: e.transpose(out=ptb[:, c, :n], in_=src_, identity=identb[:n, :n]))(),
                         reads=[("og%d" % ti) if src8 is None else "og8s", "identb"], writes=["pb%d" % b])
                for c in range(4):
                    if ti % 2 == 0:
                        S.op("dve", (lambda c=c, col0_=col0_, n=n, ptb=ptb: lambda e: e.tensor_scalar(out=HGT[:, 4 * h + c, col0_:col0_ + n], in0=ptb[:, c, :n], scalar1=hgfm[:, 4 * h + c:4 * h + c + 1], scalar2=None, op0=ALU.mult))(),
                             reads=["pb%d" % b, "hgfm"], writes=["hgT_%d_%d_%d" % (h, c, ti)])
                    else:
                        S.op("act", (lambda c=c, col0_=col0_, n=n, ptb=ptb: lambda e: e.activation(out=HGT[:, 4 * h + c, col0_:col0_ + n], in_=ptb[:, c, :n], func=AF.Copy, scale=hgfm[:, 4 * h + c:4 * h + c + 1]))(),
                             reads=["pb%d" % b, "hgfm"], writes=["hgT_%d_%d_%d" % (h, c, ti)])

        for st_ in proj_steps(0):
            st_()
        for h in range(4):
            def load_cq(q, h=h):
                Cq = CqL[q % NCQ]
                S.dma("sp", (lambda: lambda e: e.dma_start(out=Cq, in_=sc[q, h].rearrange("(c p) e -> p c e", p=128)))(), writes=["Cq%d" % (q % NCQ)])
            for q in range(NCQ):
                load_cq(q)
            chunks = [(9, 16, 0)] + [(t, 128, t + 1) for t in range(8)]
            pend = None
            for ci, (t, L, j) in enumerate(chunks):
                col0 = TILES[t][0]
                pr = ci % 2
                Cb, STb, bA, bD = CbL[pr], STbL[pr], BA[pr], BD[pr]
                cbk, stk = "Cb%d" % pr, "STb%d" % pr
                cw = cwb[:, 4 * j + h:4 * j + h + 1]
                vk, kk = "vdtm%d" % t, "ktm%d" % t
                for dc, bC in ((0, BC0), (1, BC1)):
                    mm(banks[bC][:, :], ktm[:L, t, dc * 128:(dc + 1) * 128], vdtm[:L, t, 0:512], True, True, [kk, vk], ["pb%d" % bC])
                    mm(banks[BN][:, 32 + dc:33 + dc], ktm[:L, t, dc * 128:(dc + 1) * 128], vdtm[:L, t, 512:513], True, True, [kk, vk], ["pb%d" % BN])
                S.op("act", (lambda Cb=Cb, cw=cw, h=h: lambda e: e.activation(out=Cb, in_=CST[:, h], func=AF.Copy, scale=cw))(), reads=["cst%d" % h, "cwb"], writes=[cbk])
                pend_b = None
                if pend is not None:
                    pend_b = head_out_a(*pend)
                    pend = None
                for dc, bC in ((0, BC0), (1, BC1)):
                    S.op("dve", (lambda bC=bC, h=h, dc=dc, cw=cw: lambda e: e.scalar_tensor_tensor(out=CST[:, h, dc, 0:512], in0=CST[:, h, dc, 0:512], scalar=cw, in1=banks[bC][:, :], op0=ALU.mult, op1=ALU.add))(),
                         reads=["pb%d" % bC, "cst%d" % h, "cwb"], writes=["cst%d" % h])
                S.op("dve", (lambda h=h, cw=cw: lambda e: e.scalar_tensor_tensor(out=CST[:, h, :, 512], in0=CST[:, h, :, 512], scalar=cw, in1=banks[BN][:, 32:34], op0=ALU.mult, op1=ALU.add))(),
                     reads=["pb%d" % BN, "cst%d" % h, "cwb"], writes=["cst%d" % h])
                for dc in range(2):
                    mm(banks[BS][:L, :L], kT[:, dc, col0:col0 + L], qT[:, dc, col0:col0 + L], dc == 0, dc == 1, ["kT", "qT"], ["pb%d" % BS])
                S.op("dve", (lambda STb=STb, L=L: lambda e: e.tensor_tensor(out=STb[:L, :L], in0=banks[BS][:L, :L], in1=maskU[:L, :L], op=ALU.mult))(),
                     reads=["pb%d" % BS, "maskU"], writes=[stk])
                mm(banks[bA][:L, :], STb[:L, :L], vdtm[:L, t, 0:512], True, False, [stk, vk], ["pb%d" % bA])
                for dc in range(2):
                    mm(banks[bA][:L, :], qT[:, dc, col0:col0 + L], Cb[:, dc, 0:512], False, dc == 1, ["qT", cbk], ["pb%d" % bA])
                mm(banks[bD][:L, 0:1], STb[:L, :L], vdtm[:L, t, 512:513], True, False, [stk, vk], ["pb%d" % bD])
                for dc in range(2):
                    mm(banks[bD][:L, 0:1], qT[:, dc, col0:col0 + L], Cb[:, dc, 512:513], False, dc == 1, ["qT", cbk], ["pb%d" % bD])
                if pend_b is not None:
                    head_out_b(pend_b)
                pend = (h, t, L, bA, bD, floortm[:L, t, h:h + 1])
            head_out(*pend)
            S.dma("sp", (lambda h=h: lambda e: e.dma_start(out=o_cp[h].rearrange("(c p) e -> p c e", p=128), in_=CST[:, h, :, 0:512]))(), reads=["cst%d" % h], writes=["yout"])
            S.dma("sp", (lambda h=h: lambda e: e.dma_start(out=o_np[h].rearrange("(c p) -> p c", p=128), in_=CST[:, h, :, 512]))(), reads=["cst%d" % h], writes=["yout"])
            hgt_tiles(h, [0, 1, 2, 3, 4, 5, 6, 7, 9])
            t = 8
            col0 = TILES[8][0]
            S.op("dve", lambda e: e.tensor_copy(out=qzf[:, :, 0:2040].rearrange("p c (q s) -> p c q s", s=136)[:, :, :, 0:8], in_=qT[:, :, col0:col0 + 120].rearrange("p c (q r) -> p c q r", r=8)),
                 reads=["qT", "qz"], writes=["qz"])
            S.op("dve", lambda e: e.tensor_copy(out=qzf[:, :, 2040:2048], in_=qT[:, :, col0 + 120:col0 + 128]), reads=["qT", "qz"], writes=["qz"])
            S.op("act", lambda e: e.activation(out=k8s, in_=ktm[:, 8, :], func=AF.Copy), reads=["ktm8"], writes=["k8s"])
            S.op("act", lambda e: e.activation(out=v8s[:, 0:513], in_=vdtm[:, 8, 0:513], func=AF.Copy), reads=["vdtm8"], writes=["v8s"])
            S.op("act", lambda e: e.activation(out=og8s, in_=og[:, 8, :], func=AF.Copy), reads=["og8"], writes=["og8s"])
            bA, bD = BA[0], BD[0]
            STb = STbL[0]
            for dc in range(2):
                mm(banks[BS][:, 0:128], kT[:, dc, col0:col0 + 128], qT[:, dc, col0:col0 + 128], dc == 0, dc == 1, ["kT", "qT"], ["pb%d" % BS])
            S.op("dve", lambda e: e.tensor_tensor(out=STb[:, :], in0=banks[BS][:, 0:128], in1=maskS[:, :], op=ALU.mult),
                 reads=["pb%d" % BS, "maskS"], writes=["STb0"])
            mm(banks[bA][:, :], STb[:, :], v8s[:, 0:512], True, False, ["STb0", "v8s"], ["pb%d" % bA])
            mm(banks[bD][:, 0:1], STb[:, :], v8s[:, 512:513], True, False, ["STb0", "v8s"], ["pb%d" % bD])

            def qstep(q, h=h, bA=bA, bD=bD):
                Cq, cqk = CqL[q % NCQ], "Cq%d" % (q % NCQ)
                Cqb, cqbk = CqbL[0], "Cqb0"
                kz, kzk = kzL[q % 2], "kz%d" % (q % 2)
                cs = csb[:, 4 * q + h:4 * q + h + 1]
                S.op("act", lambda e: e.activation(out=Cqb, in_=Cq, func=AF.Copy, scale=cs), reads=[cqk, "csb"], writes=[cqbk])
                S.op("dve", lambda e: e.tensor_scalar(out=kz, in0=k8s, scalar1=ind[:, q:q + 1], scalar2=None, op0=ALU.mult),
                     reads=["k8s", "ind"], writes=[kzk])
                for dc in range(2):
                    last = (q == 15 and dc == 1)
                    mm(banks[bA][:, :], qz[:, dc, q, :], Cqb[:, dc, :], False, last, ["qz", cqbk], ["pb%d" % bA])
                    mm(banks[bD][:, 0:1], qz[:, dc, q, :], nsb[:, dc, 4 * q + h:4 * q + h + 1], False, last, ["qz", "nsb"], ["pb%d" % bD])
                for dc, bC in ((0, BC0), (1, BC1)):
                    mm(banks[bC][:, :], kz[:, dc * 128:(dc + 1) * 128], v8s[:, 0:512], True, True, [kzk, "v8s"], ["pb%d" % bC])
                    mm(banks[BN][:, dc * 16 + q:dc * 16 + q + 1], kz[:, dc * 128:(dc + 1) * 128], v8s[:, 512:513], True, True, [kzk, "v8s"], ["pb%d" % BN])
                Co, cok = CoL[q % 2], "Co%d" % (q % 2)
                for dc, bC in ((0, BC0), (1, BC1)):
                    S.op("dve", (lambda bC=bC, dc=dc: lambda e: e.scalar_tensor_tensor(out=Co[:, dc, :], in0=Cq[:, dc, :], scalar=cs, in1=banks[bC][:, :], op0=ALU.mult, op1=ALU.add))(),
                         reads=["pb%d" % bC, cqk, "csb"], writes=[cok])
                S.dma("sp", lambda e: e.dma_start(out=o_cs[q, h].rearrange("(c p) e -> p c e", p=128), in_=Co), reads=[cok], writes=["yout"])
                if q + NCQ < 16:
                    load_cq(q + NCQ, h)

            qsteps = [(lambda q=q: qstep(q)) for q in range(16)]
            psteps = proj_steps(h + 1) if h < 3 else []
            bank_pool[0] = [0, 2, 4]
            np_, nq_ = len(psteps), len(qsteps)
            pi = 0
            for qi in range(nq_):
                qsteps[qi]()
                tgt = (np_ * (qi + 1)) // nq_
                while pi < tgt:
                    psteps[pi]()
                    pi += 1
            while pi < np_:
                psteps[pi]()
                pi += 1
            bank_pool[0] = list(range(8))
            S.op("dve", (lambda h=h: lambda e: e.tensor_tensor(out=nnew.rearrange("p c (q f) -> p c q f", f=4)[:, :, :, h], in0=nraw.rearrange("p c (q f) -> p c q f", f=4)[:, :, :, h], in1=banks[BN][:, 0:32].rearrange("p (c q) -> p c q", c=2), op=ALU.add))(),
                 reads=["pb%d" % BN, "nraw"], writes=["nnew"])
            head_out(h, 8, 128, bA, bD, floortm[:, 8, h:h + 1], gate=og8s, gkey="og8s")
            hgt_tiles(h, [8], src8=og8s)
        nout = hrowL[1][0:64, 0:256]
        bn2 = nbank()
        for dc in range(2):
            S.op("pe", (lambda dc=dc: lambda e: e.transpose(out=banks[bn2][0:64, dc * 128:(dc + 1) * 128], in_=nnew[:, dc, :], identity=ident[:, :]))(),
                 reads=["nnew", "ident"], writes=["pb%d" % bn2])
        S.op("dve", lambda e: e.tensor_copy(out=nout, in_=banks[bn2][0:64, 0:256]), reads=["pb%d" % bn2], writes=["hrow1"])
        S.dma("sp", lambda e: e.dma_start(out=o_ns.rearrange("q h d -> (q h) d"), in_=nout), reads=["hrow1"], writes=["yout"])
        prefetch_w("wo0", w_out0[:, 0:512], 16, 512)
        S.barrier()
        S.dma("sp", lambda e: e.dma_start(out=growM, in_=gpost[0:1, :].partition_broadcast(128)[:, 0, :]), writes=["growM"])
        proj_to_acc(HGT, "hgT", w_out0, range(10), tag="wo0")
        S.barrier()
        mode[0] = 'B'
        prefetch_w("up0", w_up[0, :, 0:512], 16, 512)
        pipeline3([boundary_stage(t, TILES[t][1], xin[128 * t:128 * t + TILES[t][1], :], hscr[128 * t:128 * t + TILES[t][1], :], None, 0, 1, XNT, TILES[t][0]) for t in range(10)])
        S.barrier()
        ffn(0, range(10))
        S.barrier()
        load_grow(1)
        prefetch_w("pin", pw_in[:, 1536:2048], 16, 512)
        pipeline3([boundary_stage(t, TILES[t][1], hscr[128 * t:128 * t + TILES[t][1], :], hscr[128 * t:128 * t + TILES[t][1], :], None, 1, 2, XNT, TILES[t][0]) for t in range(10)])
        S.barrier()

        UTgL = [ACC[:, 0:5632].rearrange("p (c n) -> p c n", c=4), ACC[:, 5632:11264].rearrange("p (c n) -> p c n", c=4)]
        SPT = [ACC[:, 11264:13312], ACC[:, 13312:15360]]
        T1 = ACC[:, 15360:16400]
        T2 = ACC[:, 16400:17440]
        GA = ACC[:, 17440:17952]
        XSO = ACC[:, 17952:18464]
        TMO = ACC[:, 18464:18976]
        PLT = R2[:].bitcast(BF16).rearrange("p (c n) -> p c n", c=16)
        for half in range(2):
            S.dma("sp", (lambda half=half: lambda e: e.dma_start(out=SPT[half][:120], in_=spool[8 * half:8 * half + 8].rearrange("q r d -> (q r) d")))(), writes=["spt"])
        S.dma("sp", lambda e: e.dma_start(out=o_pools[:, 0:7, :], in_=spool[:, 8:15, :]), writes=["yout"])
        S.op("dve", lambda e: e.memset(PLT[:, :, 0:16], 0.0), writes=["plt"])
        S.op("dve", lambda e: e.memset(T1, 0.0), writes=["T1"])
        S.op("dve", lambda e: e.memset(T2, 0.0), writes=["T2"])
        for it_, cg in enumerate((3, 2, 1, 0)):
            w = 2 ** (cg + 1)
            UTg = UTgL[it_ % 2]
            up_ = "UT%d_" % (it_ % 2)
            wv, wk_ = load_w(pw_in[:, cg * 512:(cg + 1) * 512], 16, 512, tag=("pin" if cg == 3 else None))
            for half in range(2):
                b = nbank()
                pt = banks[b][:].rearrange("p (j n) -> p j n", j=4)
                for j in range(4):
                    c = cg * 4 + j
                    S.op("pe", (lambda c=c, j=j, pt=pt, half=half: lambda e: e.transpose(out=pt[:, j, :120], in_=SPT[half][:120, c * 128:(c + 1) * 128], identity=ident[:120, :120]))(),
                         reads=["spt", "ident"], writes=["pb%d" % b])
                for j in range(4):
                    dst = UTg[:, j, 1040 + 184 * half:1040 + 184 * half + 184].rearrange("p (q r) -> p q r", r=23)[:, :, 0:15]
                    S.op("dve", (lambda j=j, pt=pt, dst=dst: lambda e: e.tensor_copy(out=dst, in_=pt[:, j, :120].rearrange("p (q r) -> p q r", r=15)))(),
                         reads=["pb%d" % b], writes=[up_ + str(j)])
            for oc in range(4):
                for (c0, ncol) in TGS:
                    b = nbank()
                    for kc in range(16):
                        mm(banks[b][:, :ncol], wv[:, kc, oc * 128:(oc + 1) * 128], XNT[:, kc, c0:c0 + ncol], kc == 0, kc == 15, ["xnT", wk_], ["pb%d" % b])
                    if c0 < 1024:
                        S.op("act", (lambda b=b, oc=oc, c0=c0, ncol=ncol, UTg=UTg: lambda e: e.activation(out=UTg[:, oc, c0:c0 + ncol], in_=banks[b][:, :ncol], func=AF.Copy))(),
                             reads=["pb%d" % b], writes=[up_ + str(oc)])
                    else:
                        S.op("act", (lambda b=b, oc=oc, UTg=UTg: lambda e: e.activation(out=UTg[:, oc, 1024:1040], in_=banks[b][:, 0:16], func=AF.Copy))(),
                             reads=["pb%d" % b], writes=[up_ + str(oc)])
                        dst = UTg[:, oc, 1040:1408].rearrange("p (q r) -> p q r", r=23)[:, :, 15:23]
                        S.op("dve", (lambda b=b, dst=dst: lambda e: e.tensor_copy(out=dst, in_=banks[b][:, 16:144].rearrange("p (q r) -> p q r", r=8)))(),
                             reads=["pb%d" % b], writes=[up_ + str(oc)])
            b = nbank()
            pt = banks[b][:].rearrange("p (j n) -> p j n", j=4)
            for j in range(4):
                S.op("pe", (lambda j=j, pt=pt, UTg=UTg: lambda e: e.transpose(out=pt[:16, j, :], in_=UTg[:, j, 1024:1040], identity=ident[:, :]))(),
                     reads=[up_ + str(j), "ident"], writes=["pb%d" % b])
            S.op("dve", (lambda cg=cg, pt=pt: lambda e: e.tensor_copy(out=XSO[:16, :], in_=pt[:16].rearrange("p j n -> p (j n)")))(),
                 reads=["pb%d" % b], writes=["xso"])
            S.dma("sp", (lambda cg=cg: lambda e: e.dma_start(out=o_poolp[:, cg * 512:(cg + 1) * 512], in_=XSO[:16, :]))(), reads=["xso"], writes=["yout"])
            b = nbank()
            pt = banks[b][:].rearrange("p (j n) -> p j n", j=4)
            for j in range(4):
                sv = UTg[:, j, 1040:1408].rearrange("p (q r) -> p q r", r=23)[:, :, 15:23]
                S.op("dve", (lambda j=j, sv=sv: lambda e: e.tensor_copy(out=GA[:, j * 128:(j + 1) * 128].rearrange("p (q r) -> p q r", r=8), in_=sv))(),
                     reads=[up_ + str(j), "GA"], writes=["GA"])
                S.op("pe", (lambda j=j, pt=pt: lambda e: e.transpose(out=pt[:, j, :], in_=GA[:, j * 128:(j + 1) * 128], identity=ident[:, :]))(),
                     reads=["GA", "ident"], writes=["pb%d" % b])
            S.op("dve", (lambda cg=cg, pt=pt: lambda e: e.tensor_copy(out=TMO[:, :], in_=pt[:].rearrange("p j n -> p (j n)")))(),
                 reads=["pb%d" % b], writes=["tmo"])
            for q in range(16):
                S.dma("sp", (lambda q=q, cg=cg: lambda e: e.dma_start(out=o_pools[q, 7:15, cg * 512:(cg + 1) * 512], in_=TMO[8 * q:8 * q + 8, :]))(), reads=["tmo"], writes=["yout"])
            for j in range(4):
                c = 4 * cg + j
                for (lo, hi, is_p) in ((0, 1040, True), (1040, 1408, False)):
                    A = UTg[:, j, lo:hi]
                    nn = hi - lo
                    cur, sh, bi = A, 1, 0
                    bufs = [T1, T2]
                    while sh < w:
                        dstb = bufs[bi]
                        S.op("dve", (lambda dstb=dstb, cur=cur, sh=sh, nn=nn: lambda e: e.tensor_tensor(out=dstb[:, sh:nn], in0=cur[:, sh:nn], in1=cur[:, 0:nn - sh], op=ALU.add))(),
                             reads=[up_ + str(j), "T1", "T2"], writes=["T1" if bi == 0 else "T2"])
                        cur = dstb
                        sh *= 2
                        bi ^= 1
                    if is_p:
                        src_s, src_u, dst = cur[:, 16:1040], A[:, 16:1040], PLT[:, c, 16:1040]
                    else:
                        src_s = cur[:, 0:368].rearrange("p (q r) -> p q r", r=23)[:, :, 15:23]
                        src_u = A.rearrange("p (q r) -> p q r", r=23)[:, :, 15:23]
                        dst = PLT[:, c, 1040:1168].rearrange("p (q r) -> p q r", r=8)
                    S.op("dve", (lambda src_s=src_s, src_u=src_u, dst=dst, w=w: lambda e: e.scalar_tensor_tensor(out=dst, in0=src_s, scalar=1.0 / w, in1=src_u, op0=ALU.mult, op1=ALU.subtract))(),
                         reads=[up_ + str(j), "T1", "T2"], writes=["plt"])
        S.barrier()
        for g in range(4):
            wgm, wgmk = load_w(pw_group[g], 4, 512)
            for oc in range(4):
                c = 4 * g + oc
                for (c0, ncol) in TGS:
                    b = nbank()
                    for kc in range(4):
                        mm(banks[b][:, :ncol], wgm[:, kc, oc * 128:(oc + 1) * 128], PLT[:, 4 * g + kc, c0:c0 + ncol], kc == 0, kc == 3, ["plt", wgmk], ["pb%d" % b])
                    S.op("act", (lambda b=b, c=c, c0=c0, ncol=ncol: lambda e: e.activation(out=XNT[:, c, c0:c0 + ncol], in_=banks[b][:, :ncol], func=AF.Copy, scale=pscale[:, c:c + 1]))(),
                         reads=["pb%d" % b, "pscale"], writes=["mixT"])
        prefetch_w("pwo", pw_out[:, 0:512], 16, 512)
        S.barrier()
        S.dma("sp", lambda e: e.dma_start(out=growM, in_=gpost[2:3, :].partition_broadcast(128)[:, 0, :]), writes=["growM"])
        proj_to_acc(XNT, "mixT", pw_out, range(9), tag="pwo")
        S.barrier()
        prefetch_w("up1", w_up[1, :, 0:512], 16, 512)
        pipeline3([boundary_stage(t, TILES[t][1], hscr[128 * t:128 * t + TILES[t][1], :], hscr[128 * t:128 * t + TILES[t][1], :], None, 2, 3, XNT, TILES[t][0]) for t in range(9)])
        S.barrier()
        ffn(1, range(9))
        S.barrier()
        load_grow(3)
        pipeline3([boundary_stage(t, TILES[t][1], hscr[128 * t:128 * t + TILES[t][1], :], None, y[128 * t:128 * t + TILES[t][1], :], 3, None, None, TILES[t][0]) for t in range(9)])
        S.barrier()
        S.op("sp", lambda e: None, reads=["yout"])
        with nc.allow_non_contiguous_dma(reason='tiny state vectors'), nc.allow_low_precision(reason='bf16 matmul operands by design'):
            S.emit()
    return nc


_NC = None


def _prep(inputs):
    f = np.float32
    x_prompt = np.asarray(inputs["x_prompt"], f)
    x_sample = np.asarray(inputs["x_sample"], f)
    meta = np.asarray(inputs["meta_tokens"], f)
    sc = np.asarray(inputs["state_mlstm_c"], f)[0]
    sn = np.asarray(inputs["state_mlstm_n"], f)[0]
    sm = np.asarray(inputs["state_mlstm_m"], f)[0]
    spool = np.asarray(inputs["state_pool"], f)[0]
    nmp, nmq = np.asarray(inputs["norm_mix_pre"], f), np.asarray(inputs["norm_mix_post"], f)
    nfp, nfq = np.asarray(inputs["norm_ffn_pre"], f), np.asarray(inputs["norm_ffn_post"], f)
    gpre = np.stack([nmp[0], nfp[0], nmp[1], nfp[1]], 0)
    gpre_fm = np.ascontiguousarray(gpre.reshape(4, 16, 128).transpose(2, 0, 1))
    gpost = np.ascontiguousarray(np.stack([nmq[0], nfq[0], nmq[1], nfq[1]], 0))
    p = np.arange(128)
    ident = np.eye(128, dtype=f)
    maskU = (p[None, :] >= p[:, None]).astype(f)
    maskS = ((p[None, :] >= p[:, None]) & ((p[None, :] // 8) == (p[:, None] // 8))).astype(f)
    indm = ((p[:, None] // 8) == np.arange(16)[None, :]).astype(f)
    shared = {
        "gpre_fm": gpre_fm, "gpost": gpost,
        "w_in0": np.ascontiguousarray(np.asarray(inputs["mlstm_w_in"], f)[0]),
        "gb_i": np.ascontiguousarray(np.asarray(inputs["mlstm_b_i"], f)[0].reshape(4, 1)),
        "gb_f": np.ascontiguousarray(np.asarray(inputs["mlstm_b_f"], f)[0].reshape(4, 1)),
        "headg": np.ascontiguousarray(np.asarray(inputs["mlstm_head_norm"], f)[0].reshape(16, 128).T),
        "w_out0": np.ascontiguousarray(np.asarray(inputs["mlstm_w_out"], f)[0]),
        "pw_in": np.ascontiguousarray(np.asarray(inputs["pool_w_in"], f)[0]),
        "pw_group": np.ascontiguousarray(np.asarray(inputs["pool_w_group"], f)[0]),
        "pscale_fm": np.ascontiguousarray(np.asarray(inputs["pool_scale"], f)[0].reshape(16, 128).T),
        "pw_out": np.ascontiguousarray(np.asarray(inputs["pool_w_out"], f)[0]),
        "w_up": np.ascontiguousarray(np.asarray(inputs["ffn_w_up"], f)),
        "w_down": np.ascontiguousarray(np.asarray(inputs["ffn_w_down"], f)),
        "c_ident": ident, "c_maskU": maskU, "c_maskS": maskS, "c_ind": indm,
    }
    maps = []
    for c in range(8):
        b, half = c // 2, c % 2
        own = x_prompt[b, 1024 * half:1024 * half + 1024]
        halo = meta if half == 0 else x_prompt[b, 1008:1024]
        samp = x_sample[16 * c:16 * c + 16].reshape(128, D)
        xin = np.ascontiguousarray(np.concatenate([own, samp, halo], 0))
        xpre = np.zeros((NPRE, D), f) if half == 0 else np.ascontiguousarray(np.concatenate([meta, x_prompt[b, 0:1008]], 0))
        m = dict(shared)
        m.update({
            "xin": xin, "xpre": xpre,
            "sc": np.ascontiguousarray(sc[16 * c:16 * c + 16]),
            "sn": np.ascontiguousarray(sn[16 * c:16 * c + 16]),
            "smT": np.ascontiguousarray(sm[16 * c:16 * c + 16].T),
            "spool": np.ascontiguousarray(spool[16 * c:16 * c + 16]),
        })
        maps.append(m)
    return maps


def kernel(**inputs):
    global _NC
    maps = _prep(inputs)
    if _NC is None:
        _NC = build_nc()
    res = run_bass_kernel_spmd(_NC, maps, core_ids=list(range(8)))
    R = res.results
    f = np.float32
    y_prompt = np.zeros((4, 2048, D), f)
    y_sample = np.zeros((128, 8, D), f)
    c_p = np.zeros((1, 4, 4, 256, 512), f)
    n_p = np.zeros((1, 4, 4, 256), f)
    m_p = np.zeros((1, 4, 4), f)
    pool_p = np.zeros((1, 4, 15, D), f)
    c_s = np.zeros((1, 128, 4, 256, 512), f)
    n_s = np.zeros((1, 128, 4, 256), f)
    m_s = np.zeros((1, 128, 4), f)
    pool_s = np.zeros((1, 128, 15, D), f)
    for c in range(8):
        b, half = c // 2, c % 2
        r = R[c]
        y_prompt[b, 1024 * half:1024 * half + 1024] = r["y"][0:1024]
        y_sample[16 * c:16 * c + 16] = r["y"][1024:1152].reshape(16, 8, D)
        c_s[0, 16 * c:16 * c + 16] = r["o_cs"]
        n_s[0, 16 * c:16 * c + 16] = r["o_ns"]
        m_s[0, 16 * c:16 * c + 16] = r["o_msT"].T
        pool_s[0, 16 * c:16 * c + 16] = r["o_pools"]
        if half == 1:
            c_p[0, b] = r["o_cp"]
            n_p[0, b] = r["o_np"]
            m_p[0, b] = r["o_mp"][:, 0]
            pool_p[0, b] = r["o_poolp"][1:16]
    return (y_prompt, y_sample, c_p, n_p, m_p, pool_p, c_s, n_s, m_s, pool_s)
```
